# Optimizing a Trainium2 kernel written in Bass

```python
import jax, jax.numpy as jnp
from jax import lax
import numpy as np

D_MODEL = 2048
BATCH = 8
SEQ = 4096
DEPTH = 2
DEC_BATCH = 16
DEC_SEQ = 32
PAST_LEN = 1024

CHUNK = 64
POOL_DIM = D_MODEL // 4
POOL_WINDOWS = (2, 4, 8, 16)
POOL_GROUPS = len(POOL_WINDOWS)
POOL_GROUP_DIM = POOL_DIM // POOL_GROUPS
POOL_HIST = max(POOL_WINDOWS) - 1
SGU_DIM = D_MODEL // 2
SGU_GROUPS = 8
SGU_GROUP_DIM = SGU_DIM // SGU_GROUPS
SGU_CHUNK = 128
CONV_DIM = D_MODEL // 4
CONV_WIDTH = 3
D_FF = -(-8 * D_MODEL // (3 * 256)) * 256
PLE_DIM = 256
N_BRANCH = 3
ALPHA = (2 * DEPTH) ** 0.25
BETA = (8 * DEPTH) ** -0.25
LN_EPS = 1e-5
OFF_U = POOL_DIM
OFF_V = OFF_U + SGU_DIM
OFF_CB = OFF_V + SGU_DIM
OFF_CC = OFF_CB + CONV_DIM
OFF_CX = OFF_CC + CONV_DIM
OFF_G = OFF_CX + CONV_DIM
IN_COLS = OFF_G + N_BRANCH * D_MODEL

kernel_name = 'hybrid_pool_sgu_conv_stream_step'


def layer_norm(x, g, b):
    xf = x.astype(jnp.float32)
    mu = jnp.mean(xf, axis=-1, keepdims=True)
    var = jnp.mean(jnp.square(xf - mu), axis=-1, keepdims=True)
    out = (xf - mu) * lax.rsqrt(var + LN_EPS) * g.astype(jnp.float32) + b.astype(jnp.float32)
    return out.astype(x.dtype)


def pool_mix(a, hist, pos0, pool_w, pool_scale):
    bsz, L, _ = a.shape
    xp = jnp.concatenate([hist, a], axis=1).astype(jnp.float32)
    cs = jnp.concatenate([jnp.zeros_like(xp[:, :1]), jnp.cumsum(xp, axis=1)], axis=1)
    end = cs[:, POOL_HIST + 1:]
    pos = pos0 + jnp.arange(L, dtype=jnp.int32) + 1
    outs = []
    for g, win in enumerate(POOL_WINDOWS):
        sl = slice(g * POOL_GROUP_DIM, (g + 1) * POOL_GROUP_DIM)
        start = cs[:, POOL_HIST + 1 - win:POOL_HIST + 1 - win + L, sl]
        cnt = jnp.minimum(pos, win).astype(jnp.float32)[None, :, None]
        outs.append((end[..., sl] - start) / cnt)
    pooled = (jnp.concatenate(outs, axis=-1) - a.astype(jnp.float32)).astype(a.dtype)
    pooled = pooled.reshape(bsz, L, POOL_GROUPS, POOL_GROUP_DIM)
    y = jnp.einsum('blgc,gcd->blgd', pooled, pool_w).reshape(bsz, L, POOL_DIM)
    return y * pool_scale


def spatial_gate(u, v, sgu_w, sgu_b):
    bsz, L, _ = v.shape
    lc = min(L, SGU_CHUNK)
    n = L // lc
    vr = v.reshape(bsz, n, lc, SGU_GROUPS, SGU_GROUP_DIM)
    w = jnp.tril(sgu_w[:, :lc, :lc])
    s = jnp.einsum('gts,bnsgc->bntgc', w, vr) + jnp.transpose(sgu_b[:, :lc])[None, None, :, :, None]
    return u * s.reshape(bsz, L, SGU_DIM)


def causal_conv(z, hist, conv_w):
    L = z.shape[1]
    zp = jnp.concatenate([hist, z], axis=1)
    y = zp[:, 0:L] * conv_w[0]
    for k in range(1, CONV_WIDTH):
        y = y + zp[:, k:k + L] * conv_w[k]
    return y


def trunk_layer(x, p, hist_pool, hist_conv, pos0, w_in, pool_w, pool_scale, sgu_ln_g, sgu_ln_b,
                sgu_w, sgu_b, conv_w, w_br_a, w_br_b, w_br_c, w_o, ln1_g, ln1_b, w_gu, w_down,
                w_pe, w_pe_gate, ln2_g, ln2_b):
    z = x @ w_in
    a_in = z[..., :OFF_U]
    u = jax.nn.gelu(z[..., OFF_U:OFF_V], approximate=False)
    v = layer_norm(jax.nn.gelu(z[..., OFF_V:OFF_CB], approximate=False), sgu_ln_g, sgu_ln_b)
    c_b = z[..., OFF_CB:OFF_CC]
    c_c = z[..., OFF_CC:OFF_CX]
    c_x = z[..., OFF_CX:OFF_G]
    g_a = jax.nn.sigmoid(z[..., OFF_G:OFF_G + D_MODEL])
    g_b = jax.nn.sigmoid(z[..., OFF_G + D_MODEL:OFF_G + 2 * D_MODEL])
    g_c = jax.nn.sigmoid(z[..., OFF_G + 2 * D_MODEL:])

    y_a = pool_mix(a_in, hist_pool, pos0, pool_w, pool_scale)
    y_b = spatial_gate(u, v, sgu_w, sgu_b)
    conv_in = c_c * c_x
    y_c = c_b * causal_conv(conv_in, hist_conv, conv_w)

    merged = g_a * (y_a @ w_br_a) + g_b * (y_b @ w_br_b) + g_c * (y_c @ w_br_c)
    x = layer_norm(ALPHA * x + merged @ w_o, ln1_g, ln1_b)

    h = x @ w_gu
    ffn = (jax.nn.silu(h[..., :D_FF]) * h[..., D_FF:]) @ w_down
    ple = jax.nn.sigmoid(x @ w_pe_gate) * (p @ w_pe)
    x = layer_norm(ALPHA * x + ffn + ple, ln2_g, ln2_b)

    new_pool = jnp.concatenate([hist_pool, a_in], axis=1)[:, -POOL_HIST:]
    new_conv = jnp.concatenate([hist_conv, conv_in], axis=1)[:, -(CONV_WIDTH - 1):]
    return x, new_pool, new_conv, v


def setup_inputs(seed: int = 0) -> dict:
    key = jax.random.key(seed)
    ks = jax.random.split(key, 32)

    def nrm(k, shape, scale=1.0):
        return jax.random.normal(k, shape, jnp.float32) * scale

    D = D_MODEL
    return {
        'x_prompt': nrm(ks[0], (BATCH, SEQ, D)),
        'x_sample': nrm(ks[1], (DEC_BATCH, DEC_SEQ, D)),
        'state_pool': nrm(ks[2], (DEPTH, DEC_BATCH, POOL_HIST, POOL_DIM)),
        'state_conv': nrm(ks[3], (DEPTH, DEC_BATCH, CONV_WIDTH - 1, CONV_DIM)),
        'p_prompt': nrm(ks[4], (DEPTH, BATCH, SEQ, PLE_DIM)),
        'p_sample': nrm(ks[5], (DEPTH, DEC_BATCH, DEC_SEQ, PLE_DIM)),
        'w_in': nrm(ks[6], (DEPTH, D, IN_COLS), D ** -0.5),
        'pool_w': nrm(ks[7], (DEPTH, POOL_GROUPS, POOL_GROUP_DIM, POOL_GROUP_DIM), POOL_GROUP_DIM ** -0.5),
        'pool_scale': 1.0 + nrm(ks[8], (DEPTH, POOL_DIM), 0.1),
        'sgu_ln_g': 1.0 + nrm(ks[9], (DEPTH, SGU_DIM), 0.01),
        'sgu_ln_b': nrm(ks[10], (DEPTH, SGU_DIM), 0.01),
        'sgu_w': nrm(ks[11], (DEPTH, SGU_GROUPS, SGU_CHUNK, SGU_CHUNK), SGU_CHUNK ** -0.5),
        'sgu_b': 1.0 + nrm(ks[12], (DEPTH, SGU_GROUPS, SGU_CHUNK), 0.1),
        'conv_w': nrm(ks[13], (DEPTH, CONV_WIDTH, CONV_DIM), CONV_WIDTH ** -0.5),
        'w_br_a': nrm(ks[14], (DEPTH, POOL_DIM, D), POOL_DIM ** -0.5),
        'w_br_b': nrm(ks[15], (DEPTH, SGU_DIM, D), SGU_DIM ** -0.5),
        'w_br_c': nrm(ks[16], (DEPTH, CONV_DIM, D), CONV_DIM ** -0.5),
        'w_o': nrm(ks[17], (DEPTH, D, D), BETA * D ** -0.5),
        'ln1_g': 1.0 + nrm(ks[18], (DEPTH, D), 0.01),
        'ln1_b': nrm(ks[19], (DEPTH, D), 0.01),
        'w_gu': nrm(ks[20], (DEPTH, D, 2 * D_FF), D ** -0.5),
        'w_down': nrm(ks[21], (DEPTH, D_FF, D), BETA * D_FF ** -0.5),
        'w_pe': nrm(ks[22], (DEPTH, PLE_DIM, D), BETA * PLE_DIM ** -0.5),
        'w_pe_gate': nrm(ks[23], (DEPTH, D, D), D ** -0.5),
        'ln2_g': 1.0 + nrm(ks[24], (DEPTH, D), 0.01),
        'ln2_b': nrm(ks[25], (DEPTH, D), 0.01),
    }


def reference(x_prompt, x_sample, state_pool, state_conv, p_prompt, p_sample, w_in, pool_w,
              pool_scale, sgu_ln_g, sgu_ln_b, sgu_w, sgu_b, conv_w, w_br_a, w_br_b, w_br_c, w_o,
              ln1_g, ln1_b, w_gu, w_down, w_pe, w_pe_gate, ln2_g, ln2_b):
    yp = x_prompt
    ys = x_sample
    zero_pool = jnp.zeros((x_prompt.shape[0], POOL_HIST, POOL_DIM), x_prompt.dtype)
    zero_conv = jnp.zeros((x_prompt.shape[0], CONV_WIDTH - 1, CONV_DIM), x_prompt.dtype)
    pool_p, conv_p, pool_s, conv_s, sgu_v_s = [], [], [], [], []
    for i in range(DEPTH):
        lw = (w_in[i], pool_w[i], pool_scale[i], sgu_ln_g[i], sgu_ln_b[i], sgu_w[i], sgu_b[i],
              conv_w[i], w_br_a[i], w_br_b[i], w_br_c[i], w_o[i], ln1_g[i], ln1_b[i], w_gu[i],
              w_down[i], w_pe[i], w_pe_gate[i], ln2_g[i], ln2_b[i])
        yp, np_pool, np_conv, _ = trunk_layer(yp, p_prompt[i], zero_pool, zero_conv, 0, *lw)
        ys, ns_pool, ns_conv, ns_v = trunk_layer(ys, p_sample[i], state_pool[i], state_conv[i],
                                                 PAST_LEN, *lw)
        pool_p.append(np_pool)
        conv_p.append(np_conv)
        pool_s.append(ns_pool)
        conv_s.append(ns_conv)
        sgu_v_s.append(ns_v)
    return (yp, ys, jnp.stack(pool_p), jnp.stack(conv_p), jnp.stack(pool_s), jnp.stack(conv_s),
            jnp.stack(sgu_v_s))
```

```python
import numpy as np
from contextlib import ExitStack
import concourse.bass as bass
import concourse.mybir as mybir
from concourse.bass_utils import run_bass_kernel_spmd

F32 = mybir.dt.float32
BF16 = mybir.dt.bfloat16
AF = mybir.ActivationFunctionType
ALU = mybir.AluOpType

D = 2048
DEPTH = 2
SEQ = 4096
DEC_SEQ = 32
POOL_DIM = 512
SGU_DIM = 1024
CONV_DIM = 512
D_FF = 5632
PLE = 256
IN_COLS = 10240
OFF_G = 4096
ALPHA = float((2 * DEPTH) ** 0.25)
LN_EPS = 1e-5
WINS = (2, 4, 8, 16)
NCORES = 8
TP = 512
NB = 5
SLOT_E = 18 * 256

ENG_SEM_ORDER = ("pe", "act", "dve")


def _esize(dt):
    return 2 if dt == BF16 else 4


class Op:
    __slots__ = ("eng", "fn", "deps", "signal", "tok", "dma_sem", "idx")

    def __init__(self, eng, fn, idx):
        self.eng = eng
        self.fn = fn
        self.deps = set()
        self.signal = False
        self.tok = None
        self.dma_sem = None
        self.idx = idx


class Prog:
    GRAN = 64

    def __init__(self):
        self.ops = []
        self.bufs = {}
        self.psum_names = set()
        self.col = {"pe": 0, "act": 1, "dve": 2, "dma": 3}

    def reg(self, name, nbytes, psum=False):
        n = (nbytes + self.GRAN - 1) // self.GRAN
        self.bufs[name] = (np.full(n, -1, np.int64), np.full((n, 4), -1, np.int64))
        if psum:
            self.psum_names.add(name)

    def _range(self, ap):
        name = ap.tensor.name
        es = _esize(ap.dtype)
        a = ap.ap
        pstride = a[0][0]
        off = int(ap.offset)
        foff = off % pstride if pstride > 0 else off
        ext = 1
        for st, cnt in a[1:]:
            ext += (cnt - 1) * abs(st)
        lo = foff * es
        hi = (foff + ext) * es
        if name in self.psum_names:
            lo, hi = 0, 2048
        return name, lo // self.GRAN, (hi + self.GRAN - 1) // self.GRAN

    def op(self, eng, fn, reads=(), writes=(), deps=(), dma=False):
        o = Op(eng, fn, len(self.ops))
        self.ops.append(o)
        col = 3 if (dma or eng not in self.col) else self.col[eng]
        d = set()
        for ap in reads:
            name, g0, g1 = self._range(ap)
            lw, lr = self.bufs[name]
            if name in self.psum_names:
                d.update(np.unique(lw[g0:g1]).tolist())
                d.update(np.unique(lr[g0:g1]).tolist())
                lw[g0:g1] = o.idx
            else:
                d.update(np.unique(lw[g0:g1]).tolist())
                if dma:
                    d.update(np.unique(lr[g0:g1, 3]).tolist())
                lr[g0:g1, col] = o.idx
        for ap in writes:
            name, g0, g1 = self._range(ap)
            lw, lr = self.bufs[name]
            d.update(np.unique(lw[g0:g1]).tolist())
            d.update(np.unique(lr[g0:g1]).tolist())
            lw[g0:g1] = o.idx
            lr[g0:g1] = -1
        d.discard(-1)
        d.discard(o.idx)
        for x in deps:
            if x is not None:
                d.add(x.idx)
        o.deps = d
        return o

    def emit(self, nc, es, handles, dma_sems):
        ops = self.ops
        for o in ops:
            best = {}
            keep = set()
            for di in o.deps:
                dop = ops[di]
                if dop.dma_sem is not None or dop.eng in ("sp", "pool"):
                    keep.add(di)
                else:
                    if dop.eng == "pe" and o.eng == "pe":
                        continue
                    if dop.eng not in best or best[dop.eng] < di:
                        best[dop.eng] = di
            keep.update(best.values())
            o.deps = keep
            for di in keep:
                ops[di].signal = True
        sems = {e: es.enter_context(nc.semaphore("sem_" + e)) for e in ENG_SEM_ORDER}
        cnt = {e: 0 for e in ENG_SEM_ORDER}
        dma_cnt = {}
        last_on_sem = {}
        for o in ops:
            if o.eng in ("sp", "pool"):
                if o.dma_sem is not None:
                    s = dma_sems[o.dma_sem]
                    dma_cnt[o.dma_sem] = dma_cnt.get(o.dma_sem, 0) + 16
                    o.tok = (s, dma_cnt[o.dma_sem])
                    prev = last_on_sem.get(o.dma_sem)
                    if prev is not None:
                        o.deps.add(prev)
                    last_on_sem[o.dma_sem] = o.idx
            elif o.signal:
                cnt[o.eng] += 1
                o.tok = (sems[o.eng], cnt[o.eng])
        per_eng = {}
        for o in ops:
            per_eng.setdefault(o.eng, []).append(o)
        block = es.enter_context(nc.Block())
        deco = {"pe": block.tensor, "act": block.scalar, "dve": block.vector, "pool": block.gpsimd, "sp": block.sync}

        def make(engname, lst):
            def body(e):
                seen = {}
                for o in lst:
                    for di in sorted(o.deps):
                        t = ops[di].tok
                        if t is None:
                            continue
                        s, v = t
                        k = id(s)
                        if seen.get(k, 0) < v:
                            e.wait_ge(s, v)
                            seen[k] = v
                    ins = o.fn(e)
                    if o.tok is not None and ins is not None:
                        if o.dma_sem is not None:
                            ins.then_inc(o.tok[0], 16)
                        else:
                            ins.then_inc(o.tok[0], 1)
            return body

        for engname, lst in per_eng.items():
            deco[engname](make(engname, lst))
        self.counts = {k: len(v) for k, v in per_eng.items()}


def build_program(n_ptiles=8, with_sample=True, depth=DEPTH, debug=False):
    nc = bass.Bass("TRN2", target_bir_lowering=False)
    NPT = n_ptiles
    NPTOK = max(NPT * TP, 128)

    def din(name, shape):
        return nc.dram_tensor(name, list(shape), F32, kind="ExternalInput").ap()

    def dout(name, shape):
        return nc.dram_tensor(name, list(shape), F32, kind="ExternalOutput").ap()

    xp = din("xp", [NPTOK, D])
    xsm = din("xs", [64, D])
    st_pool = din("st_pool", [DEPTH, 2, 15, POOL_DIM])
    st_conv = din("st_conv", [DEPTH, 2, 2, CONV_DIM])
    pp = din("pp", [DEPTH, NPTOK, PLE])
    psm = din("ps", [DEPTH, 64, PLE])
    w_in = din("w_in", [DEPTH, D, IN_COLS])
    pool_w = din("pool_w", [DEPTH, 4, 128, 128])
    pool_scale = din("pool_scale", [DEPTH, POOL_DIM])
    sgu_ln_g = din("sgu_ln_g", [DEPTH, SGU_DIM])
    sgu_ln_b = din("sgu_ln_b", [DEPTH, SGU_DIM])
    sgu_w = din("sgu_w", [DEPTH, 8, 128, 128])
    sgu_b = din("sgu_b", [DEPTH, 8, 128])
    conv_w = din("conv_w", [DEPTH, 3, CONV_DIM])
    w_br_a = din("w_br_a", [DEPTH, POOL_DIM, D])
    w_br_b = din("w_br_b", [DEPTH, SGU_DIM, D])
    w_br_c = din("w_br_c", [DEPTH, CONV_DIM, D])
    w_o = din("w_o", [DEPTH, D, D])
    ln1_g = din("ln1_g", [DEPTH, D])
    ln1_b = din("ln1_b", [DEPTH, D])
    w_gu = din("w_gu", [DEPTH, D, 2 * D_FF])
    w_down = din("w_down", [DEPTH, D_FF, D])
    w_pe = din("w_pe", [DEPTH, PLE, D])
    w_pe_gate = din("w_pe_gate", [DEPTH, D, D])
    ln2_g = din("ln2_g", [DEPTH, D])
    ln2_b = din("ln2_b", [DEPTH, D])
    c_ident = din("c_ident", [128, 128])
    c_mask = din("c_mask", [128, 128])
    c_invc = din("c_invc", [128, 64])

    yp = dout("yp", [NPTOK, D])
    ysm = dout("ys", [64, D])
    o_npp = dout("npp", [DEPTH, 15, POOL_DIM])
    o_ncp = dout("ncp", [DEPTH, 2, CONV_DIM])
    o_nps = dout("nps", [DEPTH, 2, 15, POOL_DIM])
    o_ncs = dout("ncs", [DEPTH, 2, 2, CONV_DIM])
    o_nvs = dout("nvs", [DEPTH, 64, SGU_DIM])

    P = Prog()
    es = ExitStack()

    def sb(name, shape, dt):
        t = es.enter_context(nc.sbuf_tensor(name, list(shape), dt))
        n = 1
        for s in shape[1:]:
            n *= s
        P.reg(name, n * _esize(dt))
        return t

    ident32 = sb("ident32", [128, 128], F32)
    identb = sb("identb", [128, 128], BF16)
    ones2048 = sb("ones2048", [128, 128], BF16)
    ones1024 = sb("ones1024", [128, 128], BF16)
    onesrow = sb("onesrow", [1, 128], F32)
    maskT = sb("maskT", [128, 128], F32)
    sguWT = sb("sguWT", [128, 16, 128], BF16)
    biasbc = sb("biasbc", [128, 16, 128], F32)
    poolwb = sb("poolwb", [128, 8, 128], BF16)
    params = sb("params", [128, DEPTH, 96], F32)
    invc = sb("invc", [128, 4, 16], F32)
    halo_pool = sb("halo_pool", [128, DEPTH, 4, 16], F32)
    halo_conv = sb("halo_conv", [128, DEPTH, 4, 2], F32)
    x32 = sb("x32", [128, 16, TP], F32)
    xb = sb("xb", [128, 16, TP], BF16)
    xin = sb("xin", [128, 2048], F32)
    xin2 = sb("xin2", [128, 2048], F32)
    pin = sb("pin", [128, 4, PLE], F32)
    pT = sb("pT", [128, 2, TP], BF16)
    pinf = pin[:, :, :].rearrange("p a b -> p (a b)")
    ring = sb("ring", [128, NB * SLOT_E], BF16)
    U = sb("U", [128, 44 * 512], BF16)
    MS = sb("MS", [128, 16, TP], BF16)
    lnt = sb("lnt", [128, 3, TP], F32)
    T4 = sb("T4", [128, 4, TP], F32)

    ps = []
    for i in range(8):
        t = es.enter_context(nc.psum_tensor(f"ps{i}", [128, 512], F32))
        P.reg(f"ps{i}", 2048, psum=True)
        ps.append(t)
    bank_ctr = [0]

    held = []

    def bank():
        while True:
            b = ps[bank_ctr[0] % 8]
            bank_ctr[0] += 1
            if not any(b is h for h in held):
                return b

    dma_sems = {}
    for i in range(NB):
        dma_sems[("ring", i)] = es.enter_context(nc.semaphore(f"sring{i}"))
    NSP = 16
    for i in range(NSP):
        dma_sems[("sp", i)] = es.enter_context(nc.semaphore(f"ssp{i}"))
    for i in range(4):
        dma_sems[("pl", i)] = es.enter_context(nc.semaphore(f"spl{i}"))
    sp_ctr = [0]
    pl_ctr = [0]
    out_dmas = []

    def dma_sp(out, in_, reads=(), writes=(), is_out=False):
        o = P.op("sp", lambda e: e.dma_start(out=out, in_=in_), reads=reads, writes=writes, dma=True)
        o.dma_sem = ("sp", sp_ctr[0] % NSP)
        sp_ctr[0] += 1
        if is_out:
            out_dmas.append(o)
        return o

    def dma_pool_small(out, in_, writes=()):
        o = P.op("pool", lambda e: e.dma_start(out=out, in_=in_), writes=writes, dma=True)
        o.dma_sem = ("pl", pl_ctr[0] % 4)
        pl_ctr[0] += 1
        return o

    def act(out, in_, func, bias=None, scale=None, extra_reads=()):
        kw = {}
        if bias is not None:
            kw["bias"] = bias
        if scale is not None:
            kw["scale"] = scale
        rd = [in_] + list(extra_reads)
        for v in (bias, scale):
            if v is not None and not isinstance(v, (int, float)):
                rd.append(v)
        return P.op("act", lambda e: e.activation(out=out, in_=in_, func=func, **kw), reads=rd, writes=[out])

    def tt(out, in0, in1, op):
        return P.op("dve", lambda e: e.tensor_tensor(out=out, in0=in0, in1=in1, op=op), reads=[in0, in1], writes=[out])

    def stt(out, in0, scalar, in1, op0, op1):
        rd = [in0, in1]
        if not isinstance(scalar, (int, float)):
            rd.append(scalar)
        return P.op("dve", lambda e: e.scalar_tensor_tensor(out=out, in0=in0, scalar=scalar, in1=in1, op0=op0, op1=op1),
                    reads=rd, writes=[out])

    def ts(out, in0, s1, op0):
        rd = [in0]
        if not isinstance(s1, (int, float)):
            rd.append(s1)
        return P.op("dve", lambda e: e.tensor_scalar(out=out, in0=in0, scalar1=s1, scalar2=None, op0=op0), reads=rd, writes=[out])

    def vcopy(out, in_):
        return P.op("dve", lambda e: e.tensor_copy(out=out, in_=in_), reads=[in_], writes=[out])

    def mm_group(out, pairs, reads, start_first=True, stop_last=True):
        def fn(e):
            ins = None
            n = len(pairs)
            for i, (l, r) in enumerate(pairs):
                ins = e.matmul(out, l, r, start=(start_first and i == 0), stop=(stop_last and i == n - 1))
            return ins
        return P.op("pe", fn, reads=reads, writes=[out])

    def transposes(items, reads, writes):
        def fn(e):
            ins = None
            for (o_, i_, id_) in items:
                ins = e.transpose(o_, i_, id_)
            return ins
        return P.op("pe", fn, reads=reads, writes=writes)

    blk_ctr = [0]
    NBLK_L = 156
    wscr_l = [nc.dram_tensor(f"wscr{l_}", [NBLK_L, 128, SLOT_E], BF16, kind="Internal").ap() for l_ in range(depth)]
    tile_blk = [0]
    first_tile = [True]
    tile_no_cur = [0]
    n_tiles_total = n_ptiles + (1 if with_sample else 0)
    wb_ops = {}

    def load_block(pieces):
        n = blk_ctr[0]
        blk_ctr[0] += 1
        bid = tile_blk[0]
        tile_blk[0] += 1
        s = n % NB
        base = s * SLOT_E
        views = []
        nel = 0
        for (eo, kc, ncols, src) in pieces:
            views.append(ring[:, base + eo: base + eo + kc * ncols].rearrange("p (k c) -> p k c", k=kc))
            nel = max(nel, eo + kc * ncols)
        whole = ring[:, base: base + nel]
        conv_tile = min((0, 1, 2, 1, 2, 0, 1, 2)[bid % 8], n_tiles_total - 1)
        if tile_no_cur[0] <= conv_tile:
            for (eo, kc, ncols, src), dst in zip(pieces, views):
                o = P.op("pool", (lambda d_, s_: (lambda e: e.dma_start(out=d_, in_=s_)))(dst, src), writes=[dst], dma=True)
                o.dma_sem = ("ring", s)
            if tile_no_cur[0] == conv_tile and n_tiles_total > 1:
                wb_ops[bid] = dma_sp(wscr_l[bid // NBLK_L][bid % NBLK_L, :, 0:nel], whole, reads=[whole])
        else:
            srcv = wscr_l[bid // NBLK_L][bid % NBLK_L, :, 0:nel]
            o = P.op("pool", (lambda d_, s_: (lambda e: e.dma_start(out=d_, in_=s_)))(whole, srcv), writes=[whole],
                     deps=[wb_ops[bid]], dma=True)
            o.dma_sem = ("ring", s)
        return views

    def wsrc(w_l, r0, nrows, c0, ncols):
        return w_l[r0:r0 + nrows, c0:c0 + ncols].rearrange("(k p) c -> p k c", p=128)

    o1 = dma_sp(ident32[:], c_ident[:, :], writes=[ident32[:]])
    dma_sp(maskT[:], c_mask[:, :], writes=[maskT[:]])
    dma_sp(invc[:], c_invc.rearrange("p (g t) -> p g t", g=4), writes=[invc[:]])
    vcopy(identb[:], ident32[:])
    P.op("dve", lambda e: e.memset(ones2048[:], 1.0 / 2048), writes=[ones2048[:]])
    P.op("dve", lambda e: e.memset(ones1024[:], 1.0 / 1024), writes=[ones1024[:]])
    P.op("dve", lambda e: e.memset(onesrow[:], 1.0), writes=[onesrow[:]])
    P.op("dve", lambda e: e.memset(halo_pool[:], 0.0), writes=[halo_pool[:]])
    P.op("dve", lambda e: e.memset(halo_conv[:], 0.0), writes=[halo_conv[:]])
    dma_pool_small(poolwb[:], pool_w.rearrange("l g c d -> c (l g) d"), writes=[poolwb[:]])
    stg = xin[:, :].rearrange("p (a b) -> p a b", a=16)
    dma_sp(stg, sgu_w.rearrange("l g t s -> t (l g) s"), writes=[xin[:, :]])
    for q in range(4):
        b = bank()
        transposes([(b[:, i * 128:(i + 1) * 128], stg[:, q * 4 + i, :], ident32[:]) for i in range(4)],
                   reads=[xin[:, :], ident32[:]], writes=[b[:, :]])
        tt(sguWT[:, q * 4:(q + 1) * 4, :], b[:, :].rearrange("p (a b) -> p a b", a=4),
           maskT[:].unsqueeze(1).broadcast_to([128, 4, 128]), ALU.mult)
    brow = xin[0:1, 0:2048]
    dma_sp(brow, sgu_b.rearrange("l g t -> (l g t)").unsqueeze(0), writes=[xin[:, :]])
    for q in range(4):
        b = bank()
        mm_group(b[:, :], [(onesrow[0:1, :], xin[0:1, q * 512:(q + 1) * 512])], reads=[onesrow[:], xin[:, :]])
        act(biasbc[:, q * 4:(q + 1) * 4, :], b[:, :].rearrange("p (a b) -> p a b", a=4), AF.Copy)
    PCOL = {"pool_scale": 0, "sgu_g": 4, "sgu_b": 12, "conv_w": 20, "ln1_g": 32, "ln1_b": 48, "ln2_g": 64, "ln2_b": 80}
    for l in range(depth):
        srcs = [(pool_scale, 0, 4), (sgu_ln_g, 4, 8), (sgu_ln_b, 12, 8), (ln1_g, 32, 16), (ln1_b, 48, 16),
                (ln2_g, 64, 16), (ln2_b, 80, 16)]
        for (t_, r0, nr) in srcs:
            dma_sp(xin[r0:r0 + nr, 0:128], t_[l].rearrange("(r c) -> r c", c=128), writes=[xin[:, 0:128]])
        dma_sp(xin[20:32, 0:128], conv_w[l].rearrange("k (r c) -> (k r) c", c=128), writes=[xin[:, 0:128]])
        b = bank()
        transposes([(b[:, 0:96], xin[0:96, 0:128], ident32[0:96, 0:96])], reads=[xin[:, 0:128], ident32[:]], writes=[b[:, :]])
        act(params[:, l, :], b[:, 0:96], AF.Copy)

    def pcol(l, name, k=0):
        c = PCOL[name] + k
        return params[:, l, c:c + 1]

    Uf = U[:, :]

    def uslot_bf(s0, nslots):
        return Uf[:, s0 * 512:(s0 + nslots) * 512]

    def uslot_f32(s0, nslots):
        return Uf[:, s0 * 512:(s0 + nslots) * 512].bitcast(F32)

    def dbg(name, ap):
        if not debug:
            return
        t = nc.dram_tensor("dbg_" + name, list(ap.shape), ap.dtype, kind="ExternalOutput").ap()
        dma_sp(t, ap, reads=[ap], is_out=True)

    tiles = []
    for i in range(NPT):
        tiles.append(dict(kind="p", idx=i, TT=TP, segs=[(0, TP)], L=TP, nseg=1))
    if with_sample:
        tiles.append(dict(kind="s", idx=0, TT=64, segs=[(0, 32), (32, 32)], L=32, nseg=2))

    def ln_finalize(A, B, TT, c0=0, c1=None):
        if c1 is None:
            c1 = TT
        rstd = lnt[:, 0, c0:c1]
        ta = lnt[:, 1, c0:c1]
        tb = lnt[:, 2, c0:c1]
        act(ta, A[:, c0:c1], AF.Square)
        tt(ta, B[:, c0:c1], ta, ALU.subtract)
        act(tb, ta, AF.Sqrt, bias=epsc[:, 0:1])
        P.op("dve", lambda e: e.reciprocal(out=rstd, in_=tb), reads=[tb], writes=[rstd])
        return rstd, A[:, c0:c1]

    epsc = sb("epsc", [128, 1], F32)
    P.op("dve", lambda e: e.memset(epsc[:], LN_EPS), writes=[epsc[:]])

    def ln_apply(src, gcol, bcol, rstd, Av, TT, outs, k):
        n = src.shape[-1]
        t1 = T4[:, (2 * k) % 4, :n]
        t2 = T4[:, (2 * k + 1) % 4, :n]
        tt(t1, src, Av, ALU.subtract)
        stt(t2, t1, gcol, rstd, ALU.mult, ALU.mult)
        for o_ in outs:
            act(o_, t2, AF.Identity, bias=bcol)

    def ln_full(tile_of, gname, bname, l, TT, halves, nout, ones_m, xbt, sqt, extra_out=None):
        banks = []
        for (h0, h1) in halves:
            A = bank()
            B = bank()
            banks.append((A, B))
            mm_group(A[:, h0:h1], [(ones_m[:], xbt[:, j, h0:h1]) for j in range(nout)],
                     reads=[ones_m[:]] + [xbt[:, j, h0:h1] for j in range(nout)])
            mm_group(B[:, h0:h1], [(ones_m[:], sqt[:, j, h0:h1]) for j in range(nout)],
                     reads=[ones_m[:]] + [sqt[:, j, h0:h1] for j in range(nout)])
        for (h0, h1), (A, B) in zip(halves, banks):
            rstd, Av = ln_finalize(A, B, TT, h0, h1)
            for j in range(nout):
                outs = [tile_of(j, h0, h1)]
                if extra_out is not None:
                    outs.append(extra_out(j, h0, h1))
                ln_apply(tile_of(j, h0, h1), pcol(l, gname, j), pcol(l, bname, j), rstd, Av, TT, outs, j)

    prefetched = set()
    for tile_no, tile in enumerate(tiles):
        first_tile[0] = (tile_no == 0)
        tile_no_cur[0] = tile_no
        tile_blk[0] = 0
        TT = tile["TT"]
        L = tile["L"]
        nseg = tile["nseg"]
        is_p = tile["kind"] == "p"
        ti = tile["idx"]
        tok0 = ti * TP
        x_dram = xp if is_p else xsm
        y_dram = yp if is_p else ysm
        nsub = (TT + 127) // 128
        subs = [(s * 128, min(128, TT - s * 128)) for s in range(nsub)]

        for si_, (c0, nt) in enumerate(subs):
            xst = xin if si_ % 2 == 0 else xin2
            if (tile_no, si_) not in prefetched:
                dma_sp(xst[0:nt, :], x_dram[tok0 + c0: tok0 + c0 + nt, :], writes=[xst[:, :]])
            for dg in range(4):
                b = bank()
                transposes([(b[:, i * 128: i * 128 + nt], xst[0:nt, (dg * 4 + i) * 128:(dg * 4 + i + 1) * 128], ident32[0:nt, 0:nt])
                            for i in range(4)], reads=[xst[:, :], ident32[:]], writes=[b[:, :]])
                src = b[:, :].rearrange("p (a b) -> p a b", a=4)[:, :, 0:nt]
                act(x32[:, dg * 4:(dg + 1) * 4, c0:c0 + nt], src, AF.Copy)
                vcopy(xb[:, dg * 4:(dg + 1) * 4, c0:c0 + nt], x32[:, dg * 4:(dg + 1) * 4, c0:c0 + nt])

        for l in range(depth):
            assert tile_blk[0] == l * NBLK_L, (tile_blk[0], l)
            last_layer = (l == depth - 1)
            if last_layer and tile_no + 1 < len(tiles):
                nxt = tiles[tile_no + 1]
                n_is_p = nxt["kind"] == "p"
                n_dram = xp if n_is_p else xsm
                n_tok0 = nxt["idx"] * TP
                n_subs = [(s_ * 128, min(128, nxt["TT"] - s_ * 128)) for s_ in range((nxt["TT"] + 127) // 128)]
                for si_, (c0n, ntn) in enumerate(n_subs[:2]):
                    xst = xin if si_ % 2 == 0 else xin2
                    dma_sp(xst[0:ntn, :], n_dram[n_tok0 + c0n: n_tok0 + c0n + ntn, :], writes=[xst[:, :]])
                    prefetched.add((tile_no + 1, si_))
            wl_in = w_in[l]
            p_dram = pp[l] if is_p else psm[l]
            for si, (c0, nt) in enumerate(subs):
                dma_sp(pin[0:nt, si, :], p_dram[tok0 + c0: tok0 + c0 + nt, :], writes=[pin[:, si, :]])
                b = bank()
                transposes([(b[:, k * 128:k * 128 + nt], pin[0:nt, si, k * 128:(k + 1) * 128], ident32[0:nt, 0:nt]) for k in range(2)],
                           reads=[pin[:, si, :], ident32[:]], writes=[b[:, :]])
                vcopy(pT[:, :, c0:c0 + nt], b[:, 0:256].rearrange("p (a b) -> p a b", a=2)[:, :, 0:nt])

            xb_all = xb[:, :, :TT]

            def win_block(ct0):
                return load_block([(0, 16, 256, wsrc(wl_in, 0, D, ct0 * 128, 256))])[0]

            def mm16(bk, wv, jj, rhs_t, nk=16, c0=0, c1=None):
                if c1 is None:
                    c1 = TT
                return mm_group(bk[:, c0:c1], [(wv[:, k, jj * 128:(jj + 1) * 128], rhs_t[:, k, c0:c1]) for k in range(nk)],
                                reads=[wv] + [rhs_t[:, k, c0:c1] for k in range(nk)])

            halves = [(0, TT // 2), (TT // 2, TT)] if is_p else [(0, TT)]

            gv32 = uslot_f32(0, 16).rearrange("p (a b) -> p a b", a=8)
            vnb = uslot_bf(16, 8).rearrange("p (a b) -> p a b", a=8)
            vT = uslot_bf(24, 8).rearrange("p (c g d) -> p c g d", c=4, g=8)
            y_b = uslot_bf(32, 8).rearrange("p (a b) -> p a b", a=8)
            y_a = uslot_bf(40, 4).rearrange("p (a b) -> p a b", a=4)
            sqv = MS[:, 0:8, :]
            vblocks = [win_block(12 + 2 * bi) for bi in range(4)]
            for (h0, h1) in halves:
                for bi in range(4):
                    wv = vblocks[bi]
                    for jj in range(2):
                        i = 2 * bi + jj
                        bk = bank()
                        mm16(bk, wv, jj, xb, c0=h0, c1=h1)
                        act(gv32[:, i, h0:h1], bk[:, h0:h1], AF.Gelu)
                        vcopy(vnb[:, i, h0:h1], gv32[:, i, h0:h1])
                        act(sqv[:, i, h0:h1], gv32[:, i, h0:h1], AF.Square)
            A = bank()
            mm_group(A[:, :TT], [(ones1024[:], vnb[:, i, :TT]) for i in range(8)], reads=[ones1024[:], vnb[:, :, :TT]])
            B = bank()
            mm_group(B[:, :TT], [(ones1024[:], sqv[:, i, :TT]) for i in range(8)], reads=[ones1024[:], sqv[:, :, :TT]])
            rstd, nmr = ln_finalize(A, B, TT)
            held.append(A)
            ub = MS[:, 8:16, :]
            for bi in range(4):
                wv = win_block(4 + 2 * bi)
                for jj in range(2):
                    g = 2 * bi + jj
                    bk = bank()
                    mm16(bk, wv, jj, xb)
                    act(ub[:, g, :TT], bk[:, :TT], AF.Gelu)
            for (h0, h1) in halves:
                for i in range(8):
                    outs = [vnb[:, i, h0:h1]]
                    if not is_p:
                        outs.append(gv32[:, i, h0:h1])
                    ln_apply(gv32[:, i, h0:h1], pcol(l, "sgu_g", i), pcol(l, "sgu_b", i), rstd[:, h0:h1], nmr[:, h0:h1], TT, outs, i)
            held.clear()
            if not is_p:
                for half in range(2):
                    b = bank()
                    transposes([(b[0:TT, i * 128:(i + 1) * 128], gv32[:, half * 4 + i, :TT], ident32[:]) for i in range(4)],
                               reads=[gv32[:, half * 4:(half + 1) * 4, :TT], ident32[:]], writes=[b[:, :]])
                    act(pinf[0:TT, half * 512:(half + 1) * 512], b[0:TT, :], AF.Copy)
                dma_sp(o_nvs[l], pinf[0:TT, 0:1024], reads=[pinf[:, 0:1024]], is_out=True)
            if is_p:
                chunks = [(c * 128, 128) for c in range(TT // 128)]
            else:
                chunks = [(0, 32), (32, 32)]
            for ci, (c0, nt) in enumerate(chunks):
                b = bank()
                bb = b[:, :].bitcast(BF16)
                transposes([(bb[0:nt, g * 128:(g + 1) * 128], vnb[:, g, c0:c0 + nt], identb[:]) for g in range(8)],
                           reads=[vnb[:, g, c0:c0 + nt] for g in range(8)] + [identb[:]], writes=[b[:, :]])
                vcopy(vT[0:nt, ci, :, :], bb[0:nt, :].rearrange("p (g d) -> p g d", g=8))
            if (not is_p) and l == 0:
                dbg("sguWT", sguWT[:, :, :]); dbg("biasbc", biasbc[:, :, :]); dbg("vnb", vnb[:, :, :TT]); dbg("vT", vT[0:32, 0:2, :, :])
            for g in range(8):
                bs = bank()
                def fn(e, bs=bs, g=g, chunks=chunks, vT=vT, l=l):
                    ins = None
                    for ci, (c0, nt) in enumerate(chunks):
                        ins = e.matmul(bs[:, c0:c0 + nt], vT[0:nt, ci, g, :], sguWT[0:nt, l * 8 + g, 0:nt], start=True, stop=True)
                    return ins
                P.op("pe", fn, reads=[vT[:, :, g, :], sguWT[:, l * 8 + g, :]], writes=[bs[:, :]])
                tmp = T4[:, g % 4, :TT]
                for ci, (c0, nt) in enumerate(chunks):
                    tt(tmp[:, c0:c0 + nt], bs[:, c0:c0 + nt], biasbc[:, l * 8 + g, 0:nt], ALU.add)
                tt(y_b[:, g, :TT], tmp, ub[:, g, :TT], ALU.mult)

            SW = 16 + L
            W = nseg * SW
            a32 = uslot_f32(0, 9)[:, 0:4 * W].rearrange("p (g w) -> p g w", g=4)
            a32s = uslot_f32(0, 9)[:, 0:4 * W].rearrange("p (g s w) -> p g s w", g=4, s=nseg)
            t1 = uslot_f32(9, 3)[:, 0:W]
            t2 = uslot_f32(12, 3)[:, 0:W]
            pooled = uslot_bf(15, 1)[:, 0:TT]
            if is_p:
                vcopy(a32s[:, :, 0, 0:16], halo_pool[:, l, :, :])
            else:
                for s in range(2):
                    dma_sp(pinf[0:15, 0:512], st_pool[l, s], writes=[pinf[:, 0:512]])
                    b = bank()
                    transposes([(b[:, g * 16:g * 16 + 15], pinf[0:15, g * 128:(g + 1) * 128], ident32[0:15, 0:15]) for g in range(4)],
                               reads=[pinf[:, 0:512], ident32[:]], writes=[b[:, :]])
                    vcopy(a32s[:, :, s, 1:16], b[:, 0:64].rearrange("p (g t) -> p g t", g=4)[:, :, 0:15])
            for bi in range(2):
                wv = win_block(2 * bi)
                for jj in range(2):
                    g = 2 * bi + jj
                    bk = bank()
                    mm16(bk, wv, jj, xb)
                    act(a32s[:, g, :, 16:16 + L], bk[:, :TT].rearrange("p (s t) -> p s t", s=nseg), AF.Copy)
            for g in range(4):
                win = WINS[g]
                src = a32[:, g, :]
                cur = src
                sh = 1
                k = 0
                while sh < win:
                    dst = t1 if (k % 2 == 0) else t2
                    lo = 2 * sh - 1
                    tt(dst[:, lo:W], cur[:, lo:W], cur[:, lo - sh:W - sh], ALU.add)
                    cur = dst
                    sh *= 2
                    k += 1
                curs = cur.rearrange("p (s w) -> p s w", s=nseg)[:, :, 16:16 + L]
                stt(pooled.rearrange("p (s t) -> p s t", s=nseg), curs, 1.0 / win, a32s[:, g, :, 16:16 + L], ALU.mult, ALU.subtract)
                if is_p and ti == 0:
                    tmp16 = T4[:, 0, 0:16]
                    tt(tmp16, cur[:, 16:32], invc[:, g, :], ALU.mult)
                    tt(pooled[:, 0:16], tmp16, a32[:, g, 16:32], ALU.subtract)
                bk = bank()
                mm_group(bk[:, :TT], [(poolwb[:, l * 4 + g, :], pooled)], reads=[poolwb[:, l * 4 + g, :], pooled])
                act(y_a[:, g, :TT], bk[:, :TT], AF.Copy, scale=pcol(l, "pool_scale", g))
            if is_p:
                if ti < NPT - 1:
                    vcopy(halo_pool[:, l, :, 1:16], a32s[:, :, 0, L + 1:L + 16])
                else:
                    b = bank()
                    transposes([(b[0:15, g * 128:(g + 1) * 128], a32s[:, g, 0, L + 1:L + 16], ident32[:]) for g in range(4)],
                               reads=[a32[:, :, :], ident32[:]], writes=[b[:, :]])
                    act(pinf[0:15, 0:512], b[0:15, :], AF.Copy)
                    dma_sp(o_npp[l], pinf[0:15, 0:512], reads=[pinf[:, 0:512]], is_out=True)
            else:
                for s in range(2):
                    b = bank()
                    transposes([(b[0:15, g * 128:(g + 1) * 128], a32s[:, g, s, L + 1:L + 16], ident32[:]) for g in range(4)],
                               reads=[a32[:, :, :], ident32[:]], writes=[b[:, :]])
                    act(pinf[0:15, 0:512], b[0:15, :], AF.Copy)
                    dma_sp(o_nps[l, s], pinf[0:15, 0:512], reads=[pinf[:, 0:512]], is_out=True)

            CW = 2 + L
            WC = nseg * CW
            cb32 = uslot_f32(0, 8).rearrange("p (a b) -> p a b", a=4)
            cc32 = uslot_f32(8, 8).rearrange("p (a b) -> p a b", a=4)
            ci32 = uslot_f32(16, 9)[:, 0:4 * WC].rearrange("p (c s w) -> p c s w", c=4, s=nseg)
            accs = uslot_f32(25, 4).rearrange("p (a b) -> p a b", a=2)
            y_c = uslot_bf(8, 8).rearrange("p (a two b) -> p a two b", a=4, two=2)[:, :, 0, :]
            if is_p:
                vcopy(ci32[:, :, 0, 0:2], halo_conv[:, l, :, :])
            else:
                for s in range(2):
                    dma_sp(pinf[0:2, 0:512], st_conv[l, s], writes=[pinf[:, 0:512]])
                    b = bank()
                    transposes([(b[:, c * 2:c * 2 + 2], pinf[0:2, c * 128:(c + 1) * 128], ident32[0:2, 0:2]) for c in range(4)],
                               reads=[pinf[:, 0:512], ident32[:]], writes=[b[:, :]])
                    vcopy(ci32[:, :, s, 0:2], b[:, 0:8].rearrange("p (c t) -> p c t", c=4))
            for bi in range(2):
                wv = win_block(20 + 2 * bi)
                for jj in range(2):
                    c = 2 * bi + jj
                    bk = bank()
                    mm16(bk, wv, jj, xb)
                    act(cb32[:, c, :TT], bk[:, :TT], AF.Copy)
            for bi in range(2):
                wv = win_block(24 + 2 * bi)
                for jj in range(2):
                    c = 2 * bi + jj
                    bk = bank()
                    mm16(bk, wv, jj, xb)
                    act(cc32[:, c, :TT], bk[:, :TT], AF.Copy)
            for bi in range(2):
                wv = win_block(28 + 2 * bi)
                for jj in range(2):
                    c = 2 * bi + jj
                    bk = bank()
                    mm16(bk, wv, jj, xb)
                    tt(ci32[:, c, :, 2:2 + L], bk[:, :TT].rearrange("p (s t) -> p s t", s=nseg),
                       cc32[:, c, :TT].rearrange("p (s t) -> p s t", s=nseg), ALU.mult)
                    acc = accs[:, c % 2, :TT].rearrange("p (s t) -> p s t", s=nseg)
                    ts(acc, ci32[:, c, :, 0:L], pcol(l, "conv_w", 0 * 4 + c), ALU.mult)
                    stt(acc, ci32[:, c, :, 1:1 + L], pcol(l, "conv_w", 1 * 4 + c), acc, ALU.mult, ALU.add)
                    stt(acc, ci32[:, c, :, 2:2 + L], pcol(l, "conv_w", 2 * 4 + c), acc, ALU.mult, ALU.add)
                    tt(y_c[:, c, :TT], cb32[:, c, :TT], accs[:, c % 2, :TT], ALU.mult)
            if is_p:
                if ti < NPT - 1:
                    vcopy(halo_conv[:, l, :, :], ci32[:, :, 0, L:L + 2])
                else:
                    b = bank()
                    transposes([(b[0:2, c * 128:(c + 1) * 128], ci32[:, c, 0, L:L + 2], ident32[:]) for c in range(4)],
                               reads=[uslot_f32(16, 9), ident32[:]], writes=[b[:, :]])
                    act(pinf[0:2, 0:512], b[0:2, :], AF.Copy)
                    dma_sp(o_ncp[l], pinf[0:2, 0:512], reads=[pinf[:, 0:512]], is_out=True)
            else:
                for s in range(2):
                    b = bank()
                    transposes([(b[0:2, c * 128:(c + 1) * 128], ci32[:, c, s, L:L + 2], ident32[:]) for c in range(4)],
                               reads=[uslot_f32(16, 9), ident32[:]], writes=[b[:, :]])
                    act(pinf[0:2, 0:512], b[0:2, :], AF.Copy)
                    dma_sp(o_ncs[l, s], pinf[0:2, 0:512], reads=[pinf[:, 0:512]], is_out=True)

            if (not is_p) and l == 0:
                dbg("ya", y_a[:, :, :TT]); dbg("yb", y_b[:, :, :TT]); dbg("yc", y_c[:, :, :TT])
            merged = MS
            branches = [(w_br_a[l], 4, y_a, 0), (w_br_b[l], 8, y_b, 1), (w_br_c[l], 4, y_c, 2)]
            for jp in range(8):
                for bi_, (wbr, nk, ysrc, gi) in enumerate(branches):
                    gv = load_block([(0, 16, 256, wsrc(wl_in, 0, D, OFF_G + gi * D + jp * 256, 256))])[0]
                    bv = load_block([(0, nk, 256, wsrc(wbr, 0, nk * 128, jp * 256, 256))])[0]
                    for jj in range(2):
                        j = jp * 2 + jj
                        m32 = T4[:, 2 + jj, :TT]
                        bg = bank()
                        mm16(bg, gv, jj, xb)
                        gsb = T4[:, jj, :TT]
                        act(gsb, bg[:, :TT], AF.Sigmoid)
                        bb_ = bank()
                        mm16(bb_, bv, jj, ysrc, nk=nk)
                        if bi_ == 0:
                            tt(m32, bb_[:, :TT], gsb, ALU.mult)
                        elif bi_ == 1:
                            tt(gsb, bb_[:, :TT], gsb, ALU.mult)
                            tt(m32, m32, gsb, ALU.add)
                        else:
                            tt(gsb, bb_[:, :TT], gsb, ALU.mult)
                            tt(merged[:, j, :TT], m32, gsb, ALU.add)

            if (not is_p) and l == 0:
                dbg("merged", merged[:, :, :TT])
            sq1 = uslot_bf(0, 16).rearrange("p (a b) -> p a b", a=16)
            for jp in range(8):
                wv = load_block([(0, 16, 256, wsrc(w_o[l], 0, D, jp * 256, 256))])[0]
                for jj in range(2):
                    j = jp * 2 + jj
                    bk = bank()
                    mm16(bk, wv, jj, merged)
                    stt(x32[:, j, :TT], x32[:, j, :TT], ALPHA, bk[:, :TT], ALU.mult, ALU.add)
                    vcopy(xb[:, j, :TT], x32[:, j, :TT])
                    act(sq1[:, j, :TT], x32[:, j, :TT], AF.Square)
            ln_full(lambda j, a, b: x32[:, j, a:b], "ln1_g", "ln1_b", l, TT, halves, 16, ones2048, xb, sq1,
                    extra_out=lambda j, a, b: xb[:, j, a:b])
            if (not is_p) and l == 0:
                dbg("x1", x32[:, :, :TT])
            actb = U[:, :].rearrange("p (a b) -> p a b", a=44)
            def ffn_pair(p_, wg, wu, h0, h1):
                for jj in range(2):
                    f = 2 * p_ + jj
                    bg = bank()
                    mm16(bg, wg, jj, xb, c0=h0, c1=h1)
                    sg = T4[:, f % 4, h0:h1]
                    act(sg, bg[:, h0:h1], AF.Silu)
                    bu = bank()
                    mm16(bu, wu, jj, xb, c0=h0, c1=h1)
                    tt(actb[:, f, h0:h1], bu[:, h0:h1], sg, ALU.mult)

            first_pairs = []
            for p_ in range(2):
                wg = load_block([(0, 16, 256, wsrc(w_gu[l], 0, D, p_ * 256, 256))])[0]
                wu = load_block([(0, 16, 256, wsrc(w_gu[l], 0, D, D_FF + p_ * 256, 256))])[0]
                first_pairs.append((p_, wg, wu))
            for (h0, h1) in halves:
                for (p_, wg, wu) in first_pairs:
                    ffn_pair(p_, wg, wu, h0, h1)
            for p_ in range(2, 22):
                wg = load_block([(0, 16, 256, wsrc(w_gu[l], 0, D, p_ * 256, 256))])[0]
                wu = load_block([(0, 16, 256, wsrc(w_gu[l], 0, D, D_FF + p_ * 256, 256))])[0]
                ffn_pair(p_, wg, wu, 0, TT)
            if (not is_p) and l == 0:
                dbg("act", actb[:, :, :TT])
            for jp in range(8):
                vs = load_block([(0, 16, 256, wsrc(w_pe_gate[l], 0, D, jp * 256, 256)),
                                 (16 * 256, 2, 256, wsrc(w_pe[l], 0, PLE, jp * 256, 256))])
                wpg, wpe = vs
                for jj in range(2):
                    j = jp * 2 + jj
                    bg = bank()
                    mm16(bg, wpg, jj, xb)
                    sgt = T4[:, j % 4, :TT]
                    act(sgt, bg[:, :TT], AF.Sigmoid)
                    be = bank()
                    mm16(be, wpe, jj, pT, nk=2)
                    tt(sgt, be[:, :TT], sgt, ALU.mult)
                    stt(x32[:, j, :TT], x32[:, j, :TT], ALPHA, sgt, ALU.mult, ALU.add)
            if (not is_p) and l == 0:
                dbg("xple", x32[:, :, :TT])
            sq2 = MS
            for jp in range(8):
                bks = [bank(), bank()]
                for q in range(4):
                    wv = load_block([(0, 11, 256, wsrc(w_down[l], q * 1408, 1408, jp * 256, 256))])[0]
                    def fn(e, wv=wv, q=q, bks=bks, TT=TT, actb=actb):
                        ins = None
                        for k in range(11):
                            for jj in range(2):
                                ins = e.matmul(bks[jj][:, :TT], wv[:, k, jj * 128:(jj + 1) * 128], actb[:, q * 11 + k, :TT],
                                               start=(q == 0 and k == 0), stop=(q == 3 and k == 10))
                        return ins
                    P.op("pe", fn, reads=[wv, actb[:, q * 11:(q + 1) * 11, :TT]], writes=[bks[0][:, :TT], bks[1][:, :TT]])
                for jj in range(2):
                    j = jp * 2 + jj
                    tt(x32[:, j, :TT], bks[jj][:, :TT], x32[:, j, :TT], ALU.add)
                    vcopy(xb[:, j, :TT], x32[:, j, :TT])
                    act(sq2[:, j, :TT], x32[:, j, :TT], AF.Square)
            if (not is_p) and l == 0:
                dbg("xpre2", x32[:, :, :TT])
            ln_full(lambda j, a, b: x32[:, j, a:b], "ln2_g", "ln2_b", l, TT, halves, 16, ones2048, xb, sq2,
                    extra_out=(None if last_layer else (lambda j, a, b: xb[:, j, a:b])))

        ost = [T4[:, :, :].rearrange("p a b -> p (a b)"), MS[:, :, :].rearrange("p a b -> p (a b)").bitcast(F32)[:, 0:2048]]
        for si_, (c0, nt) in enumerate(subs):
            xst = ost[si_ % 2]
            for dg in range(4):
                b = bank()
                transposes([(b[0:nt, i * 128:(i + 1) * 128], x32[:, dg * 4 + i, c0:c0 + nt], ident32[:]) for i in range(4)],
                           reads=[x32[:, dg * 4 + i, c0:c0 + nt] for i in range(4)] + [ident32[:]], writes=[b[:, :]])
                if dg % 2 == 0:
                    act(xst[0:nt, dg * 512:(dg + 1) * 512], b[0:nt, :], AF.Copy)
                else:
                    vcopy(xst[0:nt, dg * 512:(dg + 1) * 512], b[0:nt, :])
            dma_sp(y_dram[tok0 + c0: tok0 + c0 + nt, :], xst[0:nt, :], reads=[xst[:, :]], is_out=True)

    P.op("sp", lambda e: None, deps=out_dmas)
    P.emit(nc, es, None, dma_sems)
    es.close()
    return nc, P


_CACHE = {}


def _consts():
    ident = np.eye(128, dtype=np.float32)
    s = np.arange(128)[:, None]
    t = np.arange(128)[None, :]
    mask = (s <= t).astype(np.float32)
    invc = np.zeros((4, 16), np.float32)
    for g, w in enumerate(WINS):
        for tt_ in range(16):
            invc[g, tt_] = 1.0 / min(tt_ + 1, w)
    invc = np.broadcast_to(invc.reshape(1, 64), (128, 64)).copy()
    return ident, mask, invc


def kernel(**inputs):
    inp = {k: np.asarray(v) for k, v in inputs.items()}
    if "nc" not in _CACHE:
        _CACHE["nc"] = build_program()[0]
    nc = _CACHE["nc"]
    ident, mask, invc = _consts()
    wnames = ["w_in", "pool_w", "pool_scale", "sgu_ln_g", "sgu_ln_b", "sgu_w", "sgu_b", "conv_w", "w_br_a", "w_br_b",
              "w_br_c", "w_o", "ln1_g", "ln1_b", "w_gu", "w_down", "w_pe", "w_pe_gate", "ln2_g", "ln2_b"]
    shared = {k: np.ascontiguousarray(inp[k], dtype=np.float32) for k in wnames}
    in_maps = []
    for c in range(NCORES):
        m = dict(shared)
        m["xp"] = np.ascontiguousarray(inp["x_prompt"][c])
        m["xs"] = np.ascontiguousarray(inp["x_sample"][2 * c:2 * c + 2].reshape(64, D))
        m["st_pool"] = np.ascontiguousarray(inp["state_pool"][:, 2 * c:2 * c + 2])
        m["st_conv"] = np.ascontiguousarray(inp["state_conv"][:, 2 * c:2 * c + 2])
        m["pp"] = np.ascontiguousarray(inp["p_prompt"][:, c])
        m["ps"] = np.ascontiguousarray(inp["p_sample"][:, 2 * c:2 * c + 2].reshape(DEPTH, 64, PLE))
        m["c_ident"] = ident
        m["c_mask"] = mask
        m["c_invc"] = invc
        in_maps.append(m)
    res = run_bass_kernel_spmd(nc, in_maps, core_ids=list(range(NCORES)))
    rs = res.results
    y_prompt = np.stack([rs[c]["yp"] for c in range(NCORES)], 0)
    y_sample = np.concatenate([rs[c]["ys"].reshape(2, DEC_SEQ, D) for c in range(NCORES)], 0)
    npp = np.stack([rs[c]["npp"] for c in range(NCORES)], 1)
    ncp = np.stack([rs[c]["ncp"] for c in range(NCORES)], 1)
    nps = np.concatenate([rs[c]["nps"] for c in range(NCORES)], 1)
    ncs = np.concatenate([rs[c]["ncs"] for c in range(NCORES)], 1)
    nvs = np.concatenate([rs[c]["nvs"].reshape(DEPTH, 2, DEC_SEQ, SGU_DIM) for c in range(NCORES)], 1)
    return (y_prompt.astype(np.float32), y_sample.astype(np.float32), npp.astype(np.float32), ncp.astype(np.float32),
            nps.astype(np.float32), ncs.astype(np.float32), nvs.astype(np.float32))
```

```python
import numpy as np
from contextlib import ExitStack
import concourse.bass as bass
import concourse.mybir as mybir
from concourse.bass_utils import run_bass_kernel_spmd

F32 = mybir.dt.float32
BF16 = mybir.dt.bfloat16
AF = mybir.ActivationFunctionType
ALU = mybir.AluOpType

D = 2048
DEPTH = 2
SEQ = 4096
DEC_SEQ = 32
POOL_DIM = 512
SGU_DIM = 1024
CONV_DIM = 512
D_FF = 5632
PLE = 256
IN_COLS = 10240
OFF_G = 4096
ALPHA = float((2 * DEPTH) ** 0.25)
LN_EPS = 1e-5
WINS = (2, 4, 8, 16)
NCORES = 8
TP = 512
NB = 5
SLOT_E = 18 * 256

ENG_SEM_ORDER = ("pe", "act", "dve")


def _esize(dt):
    return 2 if dt == BF16 else 4


class Op:
    __slots__ = ("eng", "fn", "deps", "signal", "tok", "dma_sem", "idx")

    def __init__(self, eng, fn, idx):
        self.eng = eng
        self.fn = fn
        self.deps = set()
        self.signal = False
        self.tok = None
        self.dma_sem = None
        self.idx = idx


class Prog:
    GRAN = 64

    def __init__(self):
        self.ops = []
        self.bufs = {}
        self.psum_names = set()
        self.col = {"pe": 0, "act": 1, "dve": 2, "dma": 3}

    def reg(self, name, nbytes, psum=False):
        n = (nbytes + self.GRAN - 1) // self.GRAN
        self.bufs[name] = (np.full(n, -1, np.int64), np.full((n, 4), -1, np.int64))
        if psum:
            self.psum_names.add(name)

    def _range(self, ap):
        name = ap.tensor.name
        es = _esize(ap.dtype)
        a = ap.ap
        pstride = a[0][0]
        off = int(ap.offset)
        foff = off % pstride if pstride > 0 else off
        ext = 1
        for st, cnt in a[1:]:
            ext += (cnt - 1) * abs(st)
        lo = foff * es
        hi = (foff + ext) * es
        if name in self.psum_names:
            lo, hi = 0, 2048
        return name, lo // self.GRAN, (hi + self.GRAN - 1) // self.GRAN

    def op(self, eng, fn, reads=(), writes=(), deps=(), dma=False):
        o = Op(eng, fn, len(self.ops))
        self.ops.append(o)
        col = 3 if (dma or eng not in self.col) else self.col[eng]
        d = set()
        for ap in reads:
            name, g0, g1 = self._range(ap)
            lw, lr = self.bufs[name]
            if name in self.psum_names:
                d.update(np.unique(lw[g0:g1]).tolist())
                d.update(np.unique(lr[g0:g1]).tolist())
                lw[g0:g1] = o.idx
            else:
                d.update(np.unique(lw[g0:g1]).tolist())
                if dma:
                    d.update(np.unique(lr[g0:g1, 3]).tolist())
                lr[g0:g1, col] = o.idx
        for ap in writes:
            name, g0, g1 = self._range(ap)
            lw, lr = self.bufs[name]
            d.update(np.unique(lw[g0:g1]).tolist())
            d.update(np.unique(lr[g0:g1]).tolist())
            lw[g0:g1] = o.idx
            lr[g0:g1] = -1
        d.discard(-1)
        d.discard(o.idx)
        for x in deps:
            if x is not None:
                d.add(x.idx)
        o.deps = d
        return o

    def emit(self, nc, es, handles, dma_sems):
        ops = self.ops
        for o in ops:
            best = {}
            keep = set()
            for di in o.deps:
                dop = ops[di]
                if dop.dma_sem is not None or dop.eng in ("sp", "pool"):
                    keep.add(di)
                else:
                    if dop.eng == "pe" and o.eng == "pe":
                        continue
                    if dop.eng not in best or best[dop.eng] < di:
                        best[dop.eng] = di
            keep.update(best.values())
            o.deps = keep
            for di in keep:
                ops[di].signal = True
        sems = {e: es.enter_context(nc.semaphore("sem_" + e)) for e in ENG_SEM_ORDER}
        cnt = {e: 0 for e in ENG_SEM_ORDER}
        dma_cnt = {}
        last_on_sem = {}
        for o in ops:
            if o.eng in ("sp", "pool"):
                if o.dma_sem is not None:
                    s = dma_sems[o.dma_sem]
                    dma_cnt[o.dma_sem] = dma_cnt.get(o.dma_sem, 0) + 16
                    o.tok = (s, dma_cnt[o.dma_sem])
                    prev = last_on_sem.get(o.dma_sem)
                    if prev is not None:
                        o.deps.add(prev)
                    last_on_sem[o.dma_sem] = o.idx
            elif o.signal:
                cnt[o.eng] += 1
                o.tok = (sems[o.eng], cnt[o.eng])
        per_eng = {}
        for o in ops:
            per_eng.setdefault(o.eng, []).append(o)
        block = es.enter_context(nc.Block())
        deco = {"pe": block.tensor, "act": block.scalar, "dve": block.vector, "pool": block.gpsimd, "sp": block.sync}

        def make(engname, lst):
            def body(e):
                seen = {}
                for o in lst:
                    for di in sorted(o.deps):
                        t = ops[di].tok
                        if t is None:
                            continue
                        s, v = t
                        k = id(s)
                        if seen.get(k, 0) < v:
                            e.wait_ge(s, v)
                            seen[k] = v
                    ins = o.fn(e)
                    if o.tok is not None and ins is not None:
                        if o.dma_sem is not None:
                            ins.then_inc(o.tok[0], 16)
                        else:
                            ins.then_inc(o.tok[0], 1)
            return body

        for engname, lst in per_eng.items():
            deco[engname](make(engname, lst))
        self.counts = {k: len(v) for k, v in per_eng.items()}


def build_program(n_ptiles=8, with_sample=True, depth=DEPTH, debug=False):
    nc = bass.Bass("TRN2", target_bir_lowering=False)
    NPT = n_ptiles
    NPTOK = max(NPT * TP, 128)

    def din(name, shape):
        return nc.dram_tensor(name, list(shape), F32, kind="ExternalInput").ap()

    def dout(name, shape):
        return nc.dram_tensor(name, list(shape), F32, kind="ExternalOutput").ap()

    xp = din("xp", [NPTOK, D])
    xsm = din("xs", [64, D])
    st_pool = din("st_pool", [DEPTH, 2, 15, POOL_DIM])
    st_conv = din("st_conv", [DEPTH, 2, 2, CONV_DIM])
    pp = din("pp", [DEPTH, NPTOK, PLE])
    psm = din("ps", [DEPTH, 64, PLE])
    w_in = din("w_in", [DEPTH, D, IN_COLS])
    pool_w = din("pool_w", [DEPTH, 4, 128, 128])
    pool_scale = din("pool_scale", [DEPTH, POOL_DIM])
    sgu_ln_g = din("sgu_ln_g", [DEPTH, SGU_DIM])
    sgu_ln_b = din("sgu_ln_b", [DEPTH, SGU_DIM])
    sgu_w = din("sgu_w", [DEPTH, 8, 128, 128])
    sgu_b = din("sgu_b", [DEPTH, 8, 128])
    conv_w = din("conv_w", [DEPTH, 3, CONV_DIM])
    w_br_a = din("w_br_a", [DEPTH, POOL_DIM, D])
    w_br_b = din("w_br_b", [DEPTH, SGU_DIM, D])
    w_br_c = din("w_br_c", [DEPTH, CONV_DIM, D])
    w_o = din("w_o", [DEPTH, D, D])
    ln1_g = din("ln1_g", [DEPTH, D])
    ln1_b = din("ln1_b", [DEPTH, D])
    w_gu = din("w_gu", [DEPTH, D, 2 * D_FF])
    w_down = din("w_down", [DEPTH, D_FF, D])
    w_pe = din("w_pe", [DEPTH, PLE, D])
    w_pe_gate = din("w_pe_gate", [DEPTH, D, D])
    ln2_g = din("ln2_g", [DEPTH, D])
    ln2_b = din("ln2_b", [DEPTH, D])
    c_ident = din("c_ident", [128, 128])
    c_mask = din("c_mask", [128, 128])
    c_invc = din("c_invc", [128, 64])

    yp = dout("yp", [NPTOK, D])
    ysm = dout("ys", [64, D])
    o_npp = dout("npp", [DEPTH, 15, POOL_DIM])
    o_ncp = dout("ncp", [DEPTH, 2, CONV_DIM])
    o_nps = dout("nps", [DEPTH, 2, 15, POOL_DIM])
    o_ncs = dout("ncs", [DEPTH, 2, 2, CONV_DIM])
    o_nvs = dout("nvs", [DEPTH, 64, SGU_DIM])

    P = Prog()
    es = ExitStack()

    def sb(name, shape, dt):
        t = es.enter_context(nc.sbuf_tensor(name, list(shape), dt))
        n = 1
        for s in shape[1:]:
            n *= s
        P.reg(name, n * _esize(dt))
        return t

    ident32 = sb("ident32", [128, 128], F32)
    identb = sb("identb", [128, 128], BF16)
    ones2048 = sb("ones2048", [128, 128], BF16)
    ones1024 = sb("ones1024", [128, 128], BF16)
    onesrow = sb("onesrow", [1, 128], F32)
    maskT = sb("maskT", [128, 128], F32)
    sguWT = sb("sguWT", [128, 16, 128], BF16)
    biasbc = sb("biasbc", [128, 16, 128], F32)
    poolwb = sb("poolwb", [128, 8, 128], BF16)
    params = sb("params", [128, DEPTH, 96], F32)
    invc = sb("invc", [128, 4, 16], F32)
    halo_pool = sb("halo_pool", [128, DEPTH, 4, 16], F32)
    halo_conv = sb("halo_conv", [128, DEPTH, 4, 2], F32)
    x32 = sb("x32", [128, 16, TP], F32)
    xb = sb("xb", [128, 16, TP], BF16)
    xin = sb("xin", [128, 2048], F32)
    xin2 = sb("xin2", [128, 2048], F32)
    pin = sb("pin", [128, 4, PLE], F32)
    pT = sb("pT", [128, 2, TP], BF16)
    pinf = pin[:, :, :].rearrange("p a b -> p (a b)")
    ring = sb("ring", [128, NB * SLOT_E], BF16)
    U = sb("U", [128, 44 * 512], BF16)
    MS = sb("MS", [128, 16, TP], BF16)
    lnt = sb("lnt", [128, 3, TP], F32)
    T4 = sb("T4", [128, 4, TP], F32)

    ps = []
    for i in range(8):
        t = es.enter_context(nc.psum_tensor(f"ps{i}", [128, 512], F32))
        P.reg(f"ps{i}", 2048, psum=True)
        ps.append(t)
    bank_ctr = [0]

    held = []

    def bank():
        while True:
            b = ps[bank_ctr[0] % 8]
            bank_ctr[0] += 1
            if not any(b is h for h in held):
                return b

    dma_sems = {}
    for i in range(NB):
        dma_sems[("ring", i)] = es.enter_context(nc.semaphore(f"sring{i}"))
    NSP = 16
    for i in range(NSP):
        dma_sems[("sp", i)] = es.enter_context(nc.semaphore(f"ssp{i}"))
    for i in range(4):
        dma_sems[("pl", i)] = es.enter_context(nc.semaphore(f"spl{i}"))
    sp_ctr = [0]
    pl_ctr = [0]
    out_dmas = []

    def dma_sp(out, in_, reads=(), writes=(), is_out=False):
        o = P.op("sp", lambda e: e.dma_start(out=out, in_=in_), reads=reads, writes=writes, dma=True)
        o.dma_sem = ("sp", sp_ctr[0] % NSP)
        sp_ctr[0] += 1
        if is_out:
            out_dmas.append(o)
        return o

    def dma_pool_small(out, in_, writes=()):
        o = P.op("pool", lambda e: e.dma_start(out=out, in_=in_), writes=writes, dma=True)
        o.dma_sem = ("pl", pl_ctr[0] % 4)
        pl_ctr[0] += 1
        return o

    def act(out, in_, func, bias=None, scale=None, extra_reads=()):
        kw = {}
        if bias is not None:
            kw["bias"] = bias
        if scale is not None:
            kw["scale"] = scale
        rd = [in_] + list(extra_reads)
        for v in (bias, scale):
            if v is not None and not isinstance(v, (int, float)):
                rd.append(v)
        return P.op("act", lambda e: e.activation(out=out, in_=in_, func=func, **kw), reads=rd, writes=[out])

    def tt(out, in0, in1, op):
        return P.op("dve", lambda e: e.tensor_tensor(out=out, in0=in0, in1=in1, op=op), reads=[in0, in1], writes=[out])

    def stt(out, in0, scalar, in1, op0, op1):
        rd = [in0, in1]
        if not isinstance(scalar, (int, float)):
            rd.append(scalar)
        return P.op("dve", lambda e: e.scalar_tensor_tensor(out=out, in0=in0, scalar=scalar, in1=in1, op0=op0, op1=op1),
                    reads=rd, writes=[out])

    def ts(out, in0, s1, op0):
        rd = [in0]
        if not isinstance(s1, (int, float)):
            rd.append(s1)
        return P.op("dve", lambda e: e.tensor_scalar(out=out, in0=in0, scalar1=s1, scalar2=None, op0=op0), reads=rd, writes=[out])

    def vcopy(out, in_):
        return P.op("dve", lambda e: e.tensor_copy(out=out, in_=in_), reads=[in_], writes=[out])

    def mm_group(out, pairs, reads, start_first=True, stop_last=True):
        def fn(e):
            ins = None
            n = len(pairs)
            for i, (l, r) in enumerate(pairs):
                ins = e.matmul(out, l, r, start=(start_first and i == 0), stop=(stop_last and i == n - 1))
            return ins
        return P.op("pe", fn, reads=reads, writes=[out])

    def transposes(items, reads, writes):
        def fn(e):
            ins = None
            for (o_, i_, id_) in items:
                ins = e.transpose(o_, i_, id_)
            return ins
        return P.op("pe", fn, reads=reads, writes=writes)

    blk_ctr = [0]
    NBLK_L = 156
    wscr_l = [nc.dram_tensor(f"wscr{l_}", [NBLK_L, 128, SLOT_E], BF16, kind="Internal").ap() for l_ in range(depth)]
    tile_blk = [0]
    first_tile = [True]
    tile_no_cur = [0]
    n_tiles_total = n_ptiles + (1 if with_sample else 0)
    wb_ops = {}

    def load_block(pieces):
        n = blk_ctr[0]
        blk_ctr[0] += 1
        bid = tile_blk[0]
        tile_blk[0] += 1
        s = n % NB
        base = s * SLOT_E
        views = []
        nel = 0
        for (eo, kc, ncols, src) in pieces:
            views.append(ring[:, base + eo: base + eo + kc * ncols].rearrange("p (k c) -> p k c", k=kc))
            nel = max(nel, eo + kc * ncols)
        whole = ring[:, base: base + nel]
        conv_tile = min((0, 1, 2, 1, 2, 0, 1, 2)[bid % 8], n_tiles_total - 1)
        if tile_no_cur[0] <= conv_tile:
            for (eo, kc, ncols, src), dst in zip(pieces, views):
                o = P.op("pool", (lambda d_, s_: (lambda e: e.dma_start(out=d_, in_=s_)))(dst, src), writes=[dst], dma=True)
                o.dma_sem = ("ring", s)
            if tile_no_cur[0] == conv_tile and n_tiles_total > 1:
                wb_ops[bid] = dma_sp(wscr_l[bid // NBLK_L][bid % NBLK_L, :, 0:nel], whole, reads=[whole])
        else:
            srcv = wscr_l[bid // NBLK_L][bid % NBLK_L, :, 0:nel]
            o = P.op("pool", (lambda d_, s_: (lambda e: e.dma_start(out=d_, in_=s_)))(whole, srcv), writes=[whole],
                     deps=[wb_ops[bid]], dma=True)
            o.dma_sem = ("ring", s)
        return views

    def wsrc(w_l, r0, nrows, c0, ncols):
        return w_l[r0:r0 + nrows, c0:c0 + ncols].rearrange("(k p) c -> p k c", p=128)

    o1 = dma_sp(ident32[:], c_ident[:, :], writes=[ident32[:]])
    dma_sp(maskT[:], c_mask[:, :], writes=[maskT[:]])
    dma_sp(invc[:], c_invc.rearrange("p (g t) -> p g t", g=4), writes=[invc[:]])
    vcopy(identb[:], ident32[:])
    P.op("dve", lambda e: e.memset(ones2048[:], 1.0 / 2048), writes=[ones2048[:]])
    P.op("dve", lambda e: e.memset(ones1024[:], 1.0 / 1024), writes=[ones1024[:]])
    P.op("dve", lambda e: e.memset(onesrow[:], 1.0), writes=[onesrow[:]])
    P.op("dve", lambda e: e.memset(halo_pool[:], 0.0), writes=[halo_pool[:]])
    P.op("dve", lambda e: e.memset(halo_conv[:], 0.0), writes=[halo_conv[:]])
    dma_pool_small(poolwb[:], pool_w.rearrange("l g c d -> c (l g) d"), writes=[poolwb[:]])
    stg = xin[:, :].rearrange("p (a b) -> p a b", a=16)
    dma_sp(stg, sgu_w.rearrange("l g t s -> t (l g) s"), writes=[xin[:, :]])
    for q in range(4):
        b = bank()
        transposes([(b[:, i * 128:(i + 1) * 128], stg[:, q * 4 + i, :], ident32[:]) for i in range(4)],
                   reads=[xin[:, :], ident32[:]], writes=[b[:, :]])
        tt(sguWT[:, q * 4:(q + 1) * 4, :], b[:, :].rearrange("p (a b) -> p a b", a=4),
           maskT[:].unsqueeze(1).broadcast_to([128, 4, 128]), ALU.mult)
    brow = xin[0:1, 0:2048]
    dma_sp(brow, sgu_b.rearrange("l g t -> (l g t)").unsqueeze(0), writes=[xin[:, :]])
    for q in range(4):
        b = bank()
        mm_group(b[:, :], [(onesrow[0:1, :], xin[0:1, q * 512:(q + 1) * 512])], reads=[onesrow[:], xin[:, :]])
        act(biasbc[:, q * 4:(q + 1) * 4, :], b[:, :].rearrange("p (a b) -> p a b", a=4), AF.Copy)
    PCOL = {"pool_scale": 0, "sgu_g": 4, "sgu_b": 12, "conv_w": 20, "ln1_g": 32, "ln1_b": 48, "ln2_g": 64, "ln2_b": 80}
    for l in range(depth):
        srcs = [(pool_scale, 0, 4), (sgu_ln_g, 4, 8), (sgu_ln_b, 12, 8), (ln1_g, 32, 16), (ln1_b, 48, 16),
                (ln2_g, 64, 16), (ln2_b, 80, 16)]
        for (t_, r0, nr) in srcs:
            dma_sp(xin[r0:r0 + nr, 0:128], t_[l].rearrange("(r c) -> r c", c=128), writes=[xin[:, 0:128]])
        dma_sp(xin[20:32, 0:128], conv_w[l].rearrange("k (r c) -> (k r) c", c=128), writes=[xin[:, 0:128]])
        b = bank()
        transposes([(b[:, 0:96], xin[0:96, 0:128], ident32[0:96, 0:96])], reads=[xin[:, 0:128], ident32[:]], writes=[b[:, :]])
        act(params[:, l, :], b[:, 0:96], AF.Copy)

    def pcol(l, name, k=0):
        c = PCOL[name] + k
        return params[:, l, c:c + 1]

    Uf = U[:, :]

    def uslot_bf(s0, nslots):
        return Uf[:, s0 * 512:(s0 + nslots) * 512]

    def uslot_f32(s0, nslots):
        return Uf[:, s0 * 512:(s0 + nslots) * 512].bitcast(F32)

    def dbg(name, ap):
        if not debug:
            return
        t = nc.dram_tensor("dbg_" + name, list(ap.shape), ap.dtype, kind="ExternalOutput").ap()
        dma_sp(t, ap, reads=[ap], is_out=True)

    tiles = []
    for i in range(NPT):
        tiles.append(dict(kind="p", idx=i, TT=TP, segs=[(0, TP)], L=TP, nseg=1))
    if with_sample:
        tiles.append(dict(kind="s", idx=0, TT=64, segs=[(0, 32), (32, 32)], L=32, nseg=2))

    def ln_finalize(A, B, TT, c0=0, c1=None):
        if c1 is None:
            c1 = TT
        rstd = lnt[:, 0, c0:c1]
        ta = lnt[:, 1, c0:c1]
        tb = lnt[:, 2, c0:c1]
        act(ta, A[:, c0:c1], AF.Square)
        tt(ta, B[:, c0:c1], ta, ALU.subtract)
        act(tb, ta, AF.Sqrt, bias=epsc[:, 0:1])
        P.op("dve", lambda e: e.reciprocal(out=rstd, in_=tb), reads=[tb], writes=[rstd])
        return rstd, A[:, c0:c1]

    epsc = sb("epsc", [128, 1], F32)
    P.op("dve", lambda e: e.memset(epsc[:], LN_EPS), writes=[epsc[:]])

    def ln_apply(src, gcol, bcol, rstd, Av, TT, outs, k):
        n = src.shape[-1]
        t1 = T4[:, (2 * k) % 4, :n]
        t2 = T4[:, (2 * k + 1) % 4, :n]
        tt(t1, src, Av, ALU.subtract)
        stt(t2, t1, gcol, rstd, ALU.mult, ALU.mult)
        for o_ in outs:
            act(o_, t2, AF.Identity, bias=bcol)

    def ln_full(tile_of, gname, bname, l, TT, halves, nout, ones_m, xbt, sqt, extra_out=None):
        banks = []
        for (h0, h1) in halves:
            A = bank()
            B = bank()
            banks.append((A, B))
            mm_group(A[:, h0:h1], [(ones_m[:], xbt[:, j, h0:h1]) for j in range(nout)],
                     reads=[ones_m[:]] + [xbt[:, j, h0:h1] for j in range(nout)])
            mm_group(B[:, h0:h1], [(ones_m[:], sqt[:, j, h0:h1]) for j in range(nout)],
                     reads=[ones_m[:]] + [sqt[:, j, h0:h1] for j in range(nout)])
        for (h0, h1), (A, B) in zip(halves, banks):
            rstd, Av = ln_finalize(A, B, TT, h0, h1)
            for j in range(nout):
                outs = [tile_of(j, h0, h1)]
                if extra_out is not None:
                    outs.append(extra_out(j, h0, h1))
                ln_apply(tile_of(j, h0, h1), pcol(l, gname, j), pcol(l, bname, j), rstd, Av, TT, outs, j)

    prefetched = set()
    for tile_no, tile in enumerate(tiles):
        first_tile[0] = (tile_no == 0)
        tile_no_cur[0] = tile_no
        tile_blk[0] = 0
        TT = tile["TT"]
        L = tile["L"]
        nseg = tile["nseg"]
        is_p = tile["kind"] == "p"
        ti = tile["idx"]
        tok0 = ti * TP
        x_dram = xp if is_p else xsm
        y_dram = yp if is_p else ysm
        nsub = (TT + 127) // 128
        subs = [(s * 128, min(128, TT - s * 128)) for s in range(nsub)]

        for si_, (c0, nt) in enumerate(subs):
            xst = xin if si_ % 2 == 0 else xin2
            if (tile_no, si_) not in prefetched:
                dma_sp(xst[0:nt, :], x_dram[tok0 + c0: tok0 + c0 + nt, :], writes=[xst[:, :]])
            for dg in range(4):
                b = bank()
                transposes([(b[:, i * 128: i * 128 + nt], xst[0:nt, (dg * 4 + i) * 128:(dg * 4 + i + 1) * 128], ident32[0:nt, 0:nt])
                            for i in range(4)], reads=[xst[:, :], ident32[:]], writes=[b[:, :]])
                src = b[:, :].rearrange("p (a b) -> p a b", a=4)[:, :, 0:nt]
                act(x32[:, dg * 4:(dg + 1) * 4, c0:c0 + nt], src, AF.Copy)
                vcopy(xb[:, dg * 4:(dg + 1) * 4, c0:c0 + nt], x32[:, dg * 4:(dg + 1) * 4, c0:c0 + nt])

        for l in range(depth):
            assert tile_blk[0] == l * NBLK_L, (tile_blk[0], l)
            last_layer = (l == depth - 1)
            if last_layer and tile_no + 1 < len(tiles):
                nxt = tiles[tile_no + 1]
                n_is_p = nxt["kind"] == "p"
                n_dram = xp if n_is_p else xsm
                n_tok0 = nxt["idx"] * TP
                n_subs = [(s_ * 128, min(128, nxt["TT"] - s_ * 128)) for s_ in range((nxt["TT"] + 127) // 128)]
                for si_, (c0n, ntn) in enumerate(n_subs[:2]):
                    xst = xin if si_ % 2 == 0 else xin2
                    dma_sp(xst[0:ntn, :], n_dram[n_tok0 + c0n: n_tok0 + c0n + ntn, :], writes=[xst[:, :]])
                    prefetched.add((tile_no + 1, si_))
            wl_in = w_in[l]
            p_dram = pp[l] if is_p else psm[l]
            for si, (c0, nt) in enumerate(subs):
                dma_sp(pin[0:nt, si, :], p_dram[tok0 + c0: tok0 + c0 + nt, :], writes=[pin[:, si, :]])
                b = bank()
                transposes([(b[:, k * 128:k * 128 + nt], pin[0:nt, si, k * 128:(k + 1) * 128], ident32[0:nt, 0:nt]) for k in range(2)],
                           reads=[pin[:, si, :], ident32[:]], writes=[b[:, :]])
                vcopy(pT[:, :, c0:c0 + nt], b[:, 0:256].rearrange("p (a b) -> p a b", a=2)[:, :, 0:nt])

            xb_all = xb[:, :, :TT]

            def win_block(ct0):
                return load_block([(0, 16, 256, wsrc(wl_in, 0, D, ct0 * 128, 256))])[0]

            def mm16(bk, wv, jj, rhs_t, nk=16, c0=0, c1=None):
                if c1 is None:
                    c1 = TT
                return mm_group(bk[:, c0:c1], [(wv[:, k, jj * 128:(jj + 1) * 128], rhs_t[:, k, c0:c1]) for k in range(nk)],
                                reads=[wv] + [rhs_t[:, k, c0:c1] for k in range(nk)])

            halves = [(0, TT // 2), (TT // 2, TT)] if is_p else [(0, TT)]

            gv32 = uslot_f32(0, 16).rearrange("p (a b) -> p a b", a=8)
            vnb = uslot_bf(16, 8).rearrange("p (a b) -> p a b", a=8)
            vT = uslot_bf(24, 8).rearrange("p (c g d) -> p c g d", c=4, g=8)
            y_b = uslot_bf(32, 8).rearrange("p (a b) -> p a b", a=8)
            y_a = uslot_bf(40, 4).rearrange("p (a b) -> p a b", a=4)
            sqv = MS[:, 0:8, :]
            vblocks = [win_block(12 + 2 * bi) for bi in range(4)]
            for (h0, h1) in halves:
                for bi in range(4):
                    wv = vblocks[bi]
                    for jj in range(2):
                        i = 2 * bi + jj
                        bk = bank()
                        mm16(bk, wv, jj, xb, c0=h0, c1=h1)
                        act(gv32[:, i, h0:h1], bk[:, h0:h1], AF.Gelu)
                        vcopy(vnb[:, i, h0:h1], gv32[:, i, h0:h1])
                        act(sqv[:, i, h0:h1], gv32[:, i, h0:h1], AF.Square)
            A = bank()
            mm_group(A[:, :TT], [(ones1024[:], vnb[:, i, :TT]) for i in range(8)], reads=[ones1024[:], vnb[:, :, :TT]])
            B = bank()
            mm_group(B[:, :TT], [(ones1024[:], sqv[:, i, :TT]) for i in range(8)], reads=[ones1024[:], sqv[:, :, :TT]])
            rstd, nmr = ln_finalize(A, B, TT)
            held.append(A)
            ub = MS[:, 8:16, :]
            for bi in range(4):
                wv = win_block(4 + 2 * bi)
                for jj in range(2):
                    g = 2 * bi + jj
                    bk = bank()
                    mm16(bk, wv, jj, xb)
                    act(ub[:, g, :TT], bk[:, :TT], AF.Gelu)
            for (h0, h1) in halves:
                for i in range(8):
                    outs = [vnb[:, i, h0:h1]]
                    if not is_p:
                        outs.append(gv32[:, i, h0:h1])
                    ln_apply(gv32[:, i, h0:h1], pcol(l, "sgu_g", i), pcol(l, "sgu_b", i), rstd[:, h0:h1], nmr[:, h0:h1], TT, outs, i)
            held.clear()
            if not is_p:
                for half in range(2):
                    b = bank()
                    transposes([(b[0:TT, i * 128:(i + 1) * 128], gv32[:, half * 4 + i, :TT], ident32[:]) for i in range(4)],
                               reads=[gv32[:, half * 4:(half + 1) * 4, :TT], ident32[:]], writes=[b[:, :]])
                    act(pinf[0:TT, half * 512:(half + 1) * 512], b[0:TT, :], AF.Copy)
                dma_sp(o_nvs[l], pinf[0:TT, 0:1024], reads=[pinf[:, 0:1024]], is_out=True)
            if is_p:
                chunks = [(c * 128, 128) for c in range(TT // 128)]
            else:
                chunks = [(0, 32), (32, 32)]
            for ci, (c0, nt) in enumerate(chunks):
                b = bank()
                bb = b[:, :].bitcast(BF16)
                transposes([(bb[0:nt, g * 128:(g + 1) * 128], vnb[:, g, c0:c0 + nt], identb[:]) for g in range(8)],
                           reads=[vnb[:, g, c0:c0 + nt] for g in range(8)] + [identb[:]], writes=[b[:, :]])
                vcopy(vT[0:nt, ci, :, :], bb[0:nt, :].rearrange("p (g d) -> p g d", g=8))
            if (not is_p) and l == 0:
                dbg("sguWT", sguWT[:, :, :]); dbg("biasbc", biasbc[:, :, :]); dbg("vnb", vnb[:, :, :TT]); dbg("vT", vT[0:32, 0:2, :, :])
            for g in range(8):
                bs = bank()
                def fn(e, bs=bs, g=g, chunks=chunks, vT=vT, l=l):
                    ins = None
                    for ci, (c0, nt) in enumerate(chunks):
                        ins = e.matmul(bs[:, c0:c0 + nt], vT[0:nt, ci, g, :], sguWT[0:nt, l * 8 + g, 0:nt], start=True, stop=True)
                    return ins
                P.op("pe", fn, reads=[vT[:, :, g, :], sguWT[:, l * 8 + g, :]], writes=[bs[:, :]])
                tmp = T4[:, g % 4, :TT]
                for ci, (c0, nt) in enumerate(chunks):
                    tt(tmp[:, c0:c0 + nt], bs[:, c0:c0 + nt], biasbc[:, l * 8 + g, 0:nt], ALU.add)
                tt(y_b[:, g, :TT], tmp, ub[:, g, :TT], ALU.mult)

            SW = 16 + L
            W = nseg * SW
            a32 = uslot_f32(0, 9)[:, 0:4 * W].rearrange("p (g w) -> p g w", g=4)
            a32s = uslot_f32(0, 9)[:, 0:4 * W].rearrange("p (g s w) -> p g s w", g=4, s=nseg)
            t1 = uslot_f32(9, 3)[:, 0:W]
            t2 = uslot_f32(12, 3)[:, 0:W]
            pooled_bufs = [uslot_bf(sl_, 1)[:, 0:TT] for sl_ in (15, 29, 30, 31)]
            if is_p:
                vcopy(a32s[:, :, 0, 0:16], halo_pool[:, l, :, :])
            else:
                for s in range(2):
                    dma_sp(pinf[0:15, 0:512], st_pool[l, s], writes=[pinf[:, 0:512]])
                    b = bank()
                    transposes([(b[:, g * 16:g * 16 + 15], pinf[0:15, g * 128:(g + 1) * 128], ident32[0:15, 0:15]) for g in range(4)],
                               reads=[pinf[:, 0:512], ident32[:]], writes=[b[:, :]])
                    vcopy(a32s[:, :, s, 1:16], b[:, 0:64].rearrange("p (g t) -> p g t", g=4)[:, :, 0:15])
            for bi in range(2):
                wv = win_block(2 * bi)
                for jj in range(2):
                    g = 2 * bi + jj
                    bk = bank()
                    mm16(bk, wv, jj, xb)
                    act(a32s[:, g, :, 16:16 + L], bk[:, :TT].rearrange("p (s t) -> p s t", s=nseg), AF.Copy)
            cb32 = MS[:, 0:8, :].rearrange("p a b -> p (a b)").bitcast(F32).rearrange("p (a b) -> p a b", a=4)
            for bi in range(2):
                wv = win_block(20 + 2 * bi)
                for jj in range(2):
                    c = 2 * bi + jj
                    bk = bank()
                    mm16(bk, wv, jj, xb)
                    act(cb32[:, c, :TT], bk[:, :TT], AF.Copy)
            for g in range(4):
                win = WINS[g]
                src = a32[:, g, :]
                cur = src
                sh = 1
                k = 0
                while sh < win:
                    dst = t1 if (k % 2 == 0) else t2
                    lo = 2 * sh - 1
                    tt(dst[:, lo:W], cur[:, lo:W], cur[:, lo - sh:W - sh], ALU.add)
                    cur = dst
                    sh *= 2
                    k += 1
                curs = cur.rearrange("p (s w) -> p s w", s=nseg)[:, :, 16:16 + L]
                pooled = pooled_bufs[g]
                stt(pooled.rearrange("p (s t) -> p s t", s=nseg), curs, 1.0 / win, a32s[:, g, :, 16:16 + L], ALU.mult, ALU.subtract)
                if is_p and ti == 0:
                    tmp16 = T4[:, 0, 0:16]
                    tt(tmp16, cur[:, 16:32], invc[:, g, :], ALU.mult)
                    tt(pooled[:, 0:16], tmp16, a32[:, g, 16:32], ALU.subtract)
                bk = bank()
                mm_group(bk[:, :TT], [(poolwb[:, l * 4 + g, :], pooled)], reads=[poolwb[:, l * 4 + g, :], pooled])
                act(y_a[:, g, :TT], bk[:, :TT], AF.Copy, scale=pcol(l, "pool_scale", g))
            if is_p:
                if ti < NPT - 1:
                    vcopy(halo_pool[:, l, :, 1:16], a32s[:, :, 0, L + 1:L + 16])
                else:
                    b = bank()
                    transposes([(b[0:15, g * 128:(g + 1) * 128], a32s[:, g, 0, L + 1:L + 16], ident32[:]) for g in range(4)],
                               reads=[a32[:, :, :], ident32[:]], writes=[b[:, :]])
                    act(pinf[0:15, 0:512], b[0:15, :], AF.Copy)
                    dma_sp(o_npp[l], pinf[0:15, 0:512], reads=[pinf[:, 0:512]], is_out=True)
            else:
                for s in range(2):
                    b = bank()
                    transposes([(b[0:15, g * 128:(g + 1) * 128], a32s[:, g, s, L + 1:L + 16], ident32[:]) for g in range(4)],
                               reads=[a32[:, :, :], ident32[:]], writes=[b[:, :]])
                    act(pinf[0:15, 0:512], b[0:15, :], AF.Copy)
                    dma_sp(o_nps[l, s], pinf[0:15, 0:512], reads=[pinf[:, 0:512]], is_out=True)

            CW = 2 + L
            WC = nseg * CW
            cc32 = uslot_f32(8, 8).rearrange("p (a b) -> p a b", a=4)
            ci32 = uslot_f32(16, 9)[:, 0:4 * WC].rearrange("p (c s w) -> p c s w", c=4, s=nseg)
            accs = uslot_f32(25, 4).rearrange("p (a b) -> p a b", a=2)
            y_c = uslot_bf(8, 8).rearrange("p (a two b) -> p a two b", a=4, two=2)[:, :, 0, :]
            if is_p:
                vcopy(ci32[:, :, 0, 0:2], halo_conv[:, l, :, :])
            else:
                for s in range(2):
                    dma_sp(pinf[0:2, 0:512], st_conv[l, s], writes=[pinf[:, 0:512]])
                    b = bank()
                    transposes([(b[:, c * 2:c * 2 + 2], pinf[0:2, c * 128:(c + 1) * 128], ident32[0:2, 0:2]) for c in range(4)],
                               reads=[pinf[:, 0:512], ident32[:]], writes=[b[:, :]])
                    vcopy(ci32[:, :, s, 0:2], b[:, 0:8].rearrange("p (c t) -> p c t", c=4))
            for bi in range(2):
                wv = win_block(24 + 2 * bi)
                for jj in range(2):
                    c = 2 * bi + jj
                    bk = bank()
                    mm16(bk, wv, jj, xb)
                    act(cc32[:, c, :TT], bk[:, :TT], AF.Copy)
            for bi in range(2):
                wv = win_block(28 + 2 * bi)
                for jj in range(2):
                    c = 2 * bi + jj
                    bk = bank()
                    mm16(bk, wv, jj, xb)
                    tt(ci32[:, c, :, 2:2 + L], bk[:, :TT].rearrange("p (s t) -> p s t", s=nseg),
                       cc32[:, c, :TT].rearrange("p (s t) -> p s t", s=nseg), ALU.mult)
                    acc = accs[:, c % 2, :TT].rearrange("p (s t) -> p s t", s=nseg)
                    ts(acc, ci32[:, c, :, 0:L], pcol(l, "conv_w", 0 * 4 + c), ALU.mult)
                    stt(acc, ci32[:, c, :, 1:1 + L], pcol(l, "conv_w", 1 * 4 + c), acc, ALU.mult, ALU.add)
                    stt(acc, ci32[:, c, :, 2:2 + L], pcol(l, "conv_w", 2 * 4 + c), acc, ALU.mult, ALU.add)
                    tt(y_c[:, c, :TT], cb32[:, c, :TT], accs[:, c % 2, :TT], ALU.mult)
            if is_p:
                if ti < NPT - 1:
                    vcopy(halo_conv[:, l, :, :], ci32[:, :, 0, L:L + 2])
                else:
                    b = bank()
                    transposes([(b[0:2, c * 128:(c + 1) * 128], ci32[:, c, 0, L:L + 2], ident32[:]) for c in range(4)],
                               reads=[uslot_f32(16, 9), ident32[:]], writes=[b[:, :]])
                    act(pinf[0:2, 0:512], b[0:2, :], AF.Copy)
                    dma_sp(o_ncp[l], pinf[0:2, 0:512], reads=[pinf[:, 0:512]], is_out=True)
            else:
                for s in range(2):
                    b = bank()
                    transposes([(b[0:2, c * 128:(c + 1) * 128], ci32[:, c, s, L:L + 2], ident32[:]) for c in range(4)],
                               reads=[uslot_f32(16, 9), ident32[:]], writes=[b[:, :]])
                    act(pinf[0:2, 0:512], b[0:2, :], AF.Copy)
                    dma_sp(o_ncs[l, s], pinf[0:2, 0:512], reads=[pinf[:, 0:512]], is_out=True)

            if (not is_p) and l == 0:
                dbg("ya", y_a[:, :, :TT]); dbg("yb", y_b[:, :, :TT]); dbg("yc", y_c[:, :, :TT])
            merged = MS
            branches = [(w_br_a[l], 4, y_a, 0), (w_br_b[l], 8, y_b, 1), (w_br_c[l], 4, y_c, 2)]
            for jp in range(8):
                for bi_, (wbr, nk, ysrc, gi) in enumerate(branches):
                    gv = load_block([(0, 16, 256, wsrc(wl_in, 0, D, OFF_G + gi * D + jp * 256, 256))])[0]
                    bv = load_block([(0, nk, 256, wsrc(wbr, 0, nk * 128, jp * 256, 256))])[0]
                    for jj in range(2):
                        j = jp * 2 + jj
                        m32 = T4[:, 2 + jj, :TT]
                        bg = bank()
                        mm16(bg, gv, jj, xb)
                        gsb = T4[:, jj, :TT]
                        act(gsb, bg[:, :TT], AF.Sigmoid)
                        bb_ = bank()
                        mm16(bb_, bv, jj, ysrc, nk=nk)
                        if bi_ == 0:
                            tt(m32, bb_[:, :TT], gsb, ALU.mult)
                        elif bi_ == 1:
                            tt(gsb, bb_[:, :TT], gsb, ALU.mult)
                            tt(m32, m32, gsb, ALU.add)
                        else:
                            tt(gsb, bb_[:, :TT], gsb, ALU.mult)
                            tt(merged[:, j, :TT], m32, gsb, ALU.add)

            if (not is_p) and l == 0:
                dbg("merged", merged[:, :, :TT])
            sq1 = uslot_bf(0, 16).rearrange("p (a b) -> p a b", a=16)
            for jp in range(8):
                wv = load_block([(0, 16, 256, wsrc(w_o[l], 0, D, jp * 256, 256))])[0]
                for jj in range(2):
                    j = jp * 2 + jj
                    bk = bank()
                    mm16(bk, wv, jj, merged)
                    stt(x32[:, j, :TT], x32[:, j, :TT], ALPHA, bk[:, :TT], ALU.mult, ALU.add)
                    vcopy(xb[:, j, :TT], x32[:, j, :TT])
                    act(sq1[:, j, :TT], x32[:, j, :TT], AF.Square)
            ln_full(lambda j, a, b: x32[:, j, a:b], "ln1_g", "ln1_b", l, TT, halves, 16, ones2048, xb, sq1,
                    extra_out=lambda j, a, b: xb[:, j, a:b])
            if (not is_p) and l == 0:
                dbg("x1", x32[:, :, :TT])
            actb = U[:, :].rearrange("p (a b) -> p a b", a=44)
            def ffn_pair(p_, wg, wu, h0, h1):
                for jj in range(2):
                    f = 2 * p_ + jj
                    bg = bank()
                    mm16(bg, wg, jj, xb, c0=h0, c1=h1)
                    sg = T4[:, f % 4, h0:h1]
                    act(sg, bg[:, h0:h1], AF.Silu)
                    bu = bank()
                    mm16(bu, wu, jj, xb, c0=h0, c1=h1)
                    tt(actb[:, f, h0:h1], bu[:, h0:h1], sg, ALU.mult)

            first_pairs = []
            for p_ in range(2):
                wg = load_block([(0, 16, 256, wsrc(w_gu[l], 0, D, p_ * 256, 256))])[0]
                wu = load_block([(0, 16, 256, wsrc(w_gu[l], 0, D, D_FF + p_ * 256, 256))])[0]
                first_pairs.append((p_, wg, wu))
            for (h0, h1) in halves:
                for (p_, wg, wu) in first_pairs:
                    ffn_pair(p_, wg, wu, h0, h1)
            for p_ in range(2, 22):
                wg = load_block([(0, 16, 256, wsrc(w_gu[l], 0, D, p_ * 256, 256))])[0]
                wu = load_block([(0, 16, 256, wsrc(w_gu[l], 0, D, D_FF + p_ * 256, 256))])[0]
                ffn_pair(p_, wg, wu, 0, TT)
            if (not is_p) and l == 0:
                dbg("act", actb[:, :, :TT])
            for jp in range(8):
                vs = load_block([(0, 16, 256, wsrc(w_pe_gate[l], 0, D, jp * 256, 256)),
                                 (16 * 256, 2, 256, wsrc(w_pe[l], 0, PLE, jp * 256, 256))])
                wpg, wpe = vs
                for jj in range(2):
                    j = jp * 2 + jj
                    bg = bank()
                    mm16(bg, wpg, jj, xb)
                    sgt = T4[:, j % 4, :TT]
                    act(sgt, bg[:, :TT], AF.Sigmoid)
                    be = bank()
                    mm16(be, wpe, jj, pT, nk=2)
                    tt(sgt, be[:, :TT], sgt, ALU.mult)
                    stt(x32[:, j, :TT], x32[:, j, :TT], ALPHA, sgt, ALU.mult, ALU.add)
            if (not is_p) and l == 0:
                dbg("xple", x32[:, :, :TT])
            sq2 = MS
            for jp in range(8):
                bks = [bank(), bank()]
                for q in range(4):
                    wv = load_block([(0, 11, 256, wsrc(w_down[l], q * 1408, 1408, jp * 256, 256))])[0]
                    def fn(e, wv=wv, q=q, bks=bks, TT=TT, actb=actb):
                        ins = None
                        for k in range(11):
                            for jj in range(2):
                                ins = e.matmul(bks[jj][:, :TT], wv[:, k, jj * 128:(jj + 1) * 128], actb[:, q * 11 + k, :TT],
                                               start=(q == 0 and k == 0), stop=(q == 3 and k == 10))
                        return ins
                    P.op("pe", fn, reads=[wv, actb[:, q * 11:(q + 1) * 11, :TT]], writes=[bks[0][:, :TT], bks[1][:, :TT]])
                for jj in range(2):
                    j = jp * 2 + jj
                    tt(x32[:, j, :TT], bks[jj][:, :TT], x32[:, j, :TT], ALU.add)
                    vcopy(xb[:, j, :TT], x32[:, j, :TT])
                    act(sq2[:, j, :TT], x32[:, j, :TT], AF.Square)
            if (not is_p) and l == 0:
                dbg("xpre2", x32[:, :, :TT])
            ln_full(lambda j, a, b: x32[:, j, a:b], "ln2_g", "ln2_b", l, TT, halves, 16, ones2048, xb, sq2,
                    extra_out=(None if last_layer else (lambda j, a, b: xb[:, j, a:b])))

        ost = [T4[:, :, :].rearrange("p a b -> p (a b)"), MS[:, :, :].rearrange("p a b -> p (a b)").bitcast(F32)[:, 0:2048]]
        for si_, (c0, nt) in enumerate(subs):
            xst = ost[si_ % 2]
            for dg in range(4):
                b = bank()
                transposes([(b[0:nt, i * 128:(i + 1) * 128], x32[:, dg * 4 + i, c0:c0 + nt], ident32[:]) for i in range(4)],
                           reads=[x32[:, dg * 4 + i, c0:c0 + nt] for i in range(4)] + [ident32[:]], writes=[b[:, :]])
                if dg % 2 == 0:
                    act(xst[0:nt, dg * 512:(dg + 1) * 512], b[0:nt, :], AF.Copy)
                else:
                    vcopy(xst[0:nt, dg * 512:(dg + 1) * 512], b[0:nt, :])
            dma_sp(y_dram[tok0 + c0: tok0 + c0 + nt, :], xst[0:nt, :], reads=[xst[:, :]], is_out=True)

    P.op("sp", lambda e: None, deps=out_dmas)
    P.emit(nc, es, None, dma_sems)
    es.close()
    return nc, P


_CACHE = {}


def _consts():
    ident = np.eye(128, dtype=np.float32)
    s = np.arange(128)[:, None]
    t = np.arange(128)[None, :]
    mask = (s <= t).astype(np.float32)
    invc = np.zeros((4, 16), np.float32)
    for g, w in enumerate(WINS):
        for tt_ in range(16):
            invc[g, tt_] = 1.0 / min(tt_ + 1, w)
    invc = np.broadcast_to(invc.reshape(1, 64), (128, 64)).copy()
    return ident, mask, invc


def kernel(**inputs):
    inp = {k: np.asarray(v) for k, v in inputs.items()}
    if "nc" not in _CACHE:
        _CACHE["nc"] = build_program()[0]
    nc = _CACHE["nc"]
    ident, mask, invc = _consts()
    wnames = ["w_in", "pool_w", "pool_scale", "sgu_ln_g", "sgu_ln_b", "sgu_w", "sgu_b", "conv_w", "w_br_a", "w_br_b",
              "w_br_c", "w_o", "ln1_g", "ln1_b", "w_gu", "w_down", "w_pe", "w_pe_gate", "ln2_g", "ln2_b"]
    shared = {k: np.ascontiguousarray(inp[k], dtype=np.float32) for k in wnames}
    in_maps = []
    for c in range(NCORES):
        m = dict(shared)
        m["xp"] = np.ascontiguousarray(inp["x_prompt"][c])
        m["xs"] = np.ascontiguousarray(inp["x_sample"][2 * c:2 * c + 2].reshape(64, D))
        m["st_pool"] = np.ascontiguousarray(inp["state_pool"][:, 2 * c:2 * c + 2])
        m["st_conv"] = np.ascontiguousarray(inp["state_conv"][:, 2 * c:2 * c + 2])
        m["pp"] = np.ascontiguousarray(inp["p_prompt"][:, c])
        m["ps"] = np.ascontiguousarray(inp["p_sample"][:, 2 * c:2 * c + 2].reshape(DEPTH, 64, PLE))
        m["c_ident"] = ident
        m["c_mask"] = mask
        m["c_invc"] = invc
        in_maps.append(m)
    res = run_bass_kernel_spmd(nc, in_maps, core_ids=list(range(NCORES)))
    rs = res.results
    y_prompt = np.stack([rs[c]["yp"] for c in range(NCORES)], 0)
    y_sample = np.concatenate([rs[c]["ys"].reshape(2, DEC_SEQ, D) for c in range(NCORES)], 0)
    npp = np.stack([rs[c]["npp"] for c in range(NCORES)], 1)
    ncp = np.stack([rs[c]["ncp"] for c in range(NCORES)], 1)
    nps = np.concatenate([rs[c]["nps"] for c in range(NCORES)], 1)
    ncs = np.concatenate([rs[c]["ncs"] for c in range(NCORES)], 1)
    nvs = np.concatenate([rs[c]["nvs"].reshape(DEPTH, 2, DEC_SEQ, SGU_DIM) for c in range(NCORES)], 1)
    return (y_prompt.astype(np.float32), y_sample.astype(np.float32), npp.astype(np.float32), ncp.astype(np.float32),
            nps.astype(np.float32), ncs.astype(np.float32), nvs.astype(np.float32))
```

```python
import numpy as np
from contextlib import ExitStack
import concourse.bass as bass
import concourse.mybir as mybir
from concourse.bass_utils import run_bass_kernel_spmd

F32 = mybir.dt.float32
BF16 = mybir.dt.bfloat16
AF = mybir.ActivationFunctionType
ALU = mybir.AluOpType

D = 2048
DEPTH = 2
SEQ = 4096
DEC_SEQ = 32
POOL_DIM = 512
SGU_DIM = 1024
CONV_DIM = 512
D_FF = 5632
PLE = 256
IN_COLS = 10240
OFF_G = 4096
ALPHA = float((2 * DEPTH) ** 0.25)
LN_EPS = 1e-5
WINS = (2, 4, 8, 16)
NCORES = 8
TP = 512
NB = 5
SLOT_E = 18 * 256

ENG_SEM_ORDER = ("pe", "act", "dve")


def _esize(dt):
    return 2 if dt == BF16 else 4


class Op:
    __slots__ = ("eng", "fn", "deps", "signal", "tok", "dma_sem", "idx")

    def __init__(self, eng, fn, idx):
        self.eng = eng
        self.fn = fn
        self.deps = set()
        self.signal = False
        self.tok = None
        self.dma_sem = None
        self.idx = idx


class Prog:
    GRAN = 64

    def __init__(self):
        self.ops = []
        self.bufs = {}
        self.psum_names = set()
        self.col = {"pe": 0, "act": 1, "dve": 2, "dma": 3}

    def reg(self, name, nbytes, psum=False):
        n = (nbytes + self.GRAN - 1) // self.GRAN
        self.bufs[name] = (np.full(n, -1, np.int64), np.full((n, 4), -1, np.int64))
        if psum:
            self.psum_names.add(name)

    def _range(self, ap):
        name = ap.tensor.name
        es = _esize(ap.dtype)
        a = ap.ap
        pstride = a[0][0]
        off = int(ap.offset)
        foff = off % pstride if pstride > 0 else off
        ext = 1
        for st, cnt in a[1:]:
            ext += (cnt - 1) * abs(st)
        lo = foff * es
        hi = (foff + ext) * es
        if name in self.psum_names:
            lo, hi = 0, 2048
        return name, lo // self.GRAN, (hi + self.GRAN - 1) // self.GRAN

    def op(self, eng, fn, reads=(), writes=(), deps=(), dma=False):
        o = Op(eng, fn, len(self.ops))
        self.ops.append(o)
        col = 3 if (dma or eng not in self.col) else self.col[eng]
        d = set()
        for ap in reads:
            name, g0, g1 = self._range(ap)
            lw, lr = self.bufs[name]
            if name in self.psum_names:
                d.update(np.unique(lw[g0:g1]).tolist())
                d.update(np.unique(lr[g0:g1]).tolist())
                lw[g0:g1] = o.idx
            else:
                d.update(np.unique(lw[g0:g1]).tolist())
                if dma:
                    d.update(np.unique(lr[g0:g1, 3]).tolist())
                lr[g0:g1, col] = o.idx
        for ap in writes:
            name, g0, g1 = self._range(ap)
            lw, lr = self.bufs[name]
            d.update(np.unique(lw[g0:g1]).tolist())
            d.update(np.unique(lr[g0:g1]).tolist())
            lw[g0:g1] = o.idx
            lr[g0:g1] = -1
        d.discard(-1)
        d.discard(o.idx)
        for x in deps:
            if x is not None:
                d.add(x.idx)
        o.deps = d
        return o

    def emit(self, nc, es, handles, dma_sems):
        ops = self.ops
        for o in ops:
            best = {}
            keep = set()
            for di in o.deps:
                dop = ops[di]
                if dop.dma_sem is not None or dop.eng in ("sp", "pool"):
                    keep.add(di)
                else:
                    if dop.eng == "pe" and o.eng == "pe":
                        continue
                    if dop.eng not in best or best[dop.eng] < di:
                        best[dop.eng] = di
            keep.update(best.values())
            o.deps = keep
            for di in keep:
                ops[di].signal = True
        sems = {e: es.enter_context(nc.semaphore("sem_" + e)) for e in ENG_SEM_ORDER}
        cnt = {e: 0 for e in ENG_SEM_ORDER}
        dma_cnt = {}
        last_on_sem = {}
        for o in ops:
            if o.eng in ("sp", "pool"):
                if o.dma_sem is not None:
                    s = dma_sems[o.dma_sem]
                    dma_cnt[o.dma_sem] = dma_cnt.get(o.dma_sem, 0) + 16
                    o.tok = (s, dma_cnt[o.dma_sem])
                    prev = last_on_sem.get(o.dma_sem)
                    if prev is not None:
                        o.deps.add(prev)
                    last_on_sem[o.dma_sem] = o.idx
            elif o.signal:
                cnt[o.eng] += 1
                o.tok = (sems[o.eng], cnt[o.eng])
        per_eng = {}
        for o in ops:
            per_eng.setdefault(o.eng, []).append(o)
        block = es.enter_context(nc.Block())
        deco = {"pe": block.tensor, "act": block.scalar, "dve": block.vector, "pool": block.gpsimd, "sp": block.sync}

        def make(engname, lst):
            def body(e):
                seen = {}
                for o in lst:
                    for di in sorted(o.deps):
                        t = ops[di].tok
                        if t is None:
                            continue
                        s, v = t
                        k = id(s)
                        if seen.get(k, 0) < v:
                            e.wait_ge(s, v)
                            seen[k] = v
                    ins = o.fn(e)
                    if o.tok is not None and ins is not None:
                        if o.dma_sem is not None:
                            ins.then_inc(o.tok[0], 16)
                        else:
                            ins.then_inc(o.tok[0], 1)
            return body

        for engname, lst in per_eng.items():
            deco[engname](make(engname, lst))
        self.counts = {k: len(v) for k, v in per_eng.items()}


def build_program(n_ptiles=8, with_sample=True, depth=DEPTH, debug=False):
    nc = bass.Bass("TRN2", target_bir_lowering=False)
    NPT = n_ptiles
    NPTOK = max(NPT * TP, 128)

    def din(name, shape):
        return nc.dram_tensor(name, list(shape), F32, kind="ExternalInput").ap()

    def dout(name, shape):
        return nc.dram_tensor(name, list(shape), F32, kind="ExternalOutput").ap()

    xp = din("xp", [NPTOK, D])
    xsm = din("xs", [64, D])
    st_pool = din("st_pool", [DEPTH, 2, 15, POOL_DIM])
    st_conv = din("st_conv", [DEPTH, 2, 2, CONV_DIM])
    pp = din("pp", [DEPTH, NPTOK, PLE])
    psm = din("ps", [DEPTH, 64, PLE])
    w_in = din("w_in", [DEPTH, D, IN_COLS])
    pool_w = din("pool_w", [DEPTH, 4, 128, 128])
    pool_scale = din("pool_scale", [DEPTH, POOL_DIM])
    sgu_ln_g = din("sgu_ln_g", [DEPTH, SGU_DIM])
    sgu_ln_b = din("sgu_ln_b", [DEPTH, SGU_DIM])
    sgu_w = din("sgu_w", [DEPTH, 8, 128, 128])
    sgu_b = din("sgu_b", [DEPTH, 8, 128])
    conv_w = din("conv_w", [DEPTH, 3, CONV_DIM])
    w_br_a = din("w_br_a", [DEPTH, POOL_DIM, D])
    w_br_b = din("w_br_b", [DEPTH, SGU_DIM, D])
    w_br_c = din("w_br_c", [DEPTH, CONV_DIM, D])
    w_o = din("w_o", [DEPTH, D, D])
    ln1_g = din("ln1_g", [DEPTH, D])
    ln1_b = din("ln1_b", [DEPTH, D])
    w_gu = din("w_gu", [DEPTH, D, 2 * D_FF])
    w_down = din("w_down", [DEPTH, D_FF, D])
    w_pe = din("w_pe", [DEPTH, PLE, D])
    w_pe_gate = din("w_pe_gate", [DEPTH, D, D])
    ln2_g = din("ln2_g", [DEPTH, D])
    ln2_b = din("ln2_b", [DEPTH, D])
    c_ident = din("c_ident", [128, 128])
    c_mask = din("c_mask", [128, 128])
    c_invc = din("c_invc", [128, 64])

    yp = dout("yp", [NPTOK, D])
    ysm = dout("ys", [64, D])
    o_npp = dout("npp", [DEPTH, 15, POOL_DIM])
    o_ncp = dout("ncp", [DEPTH, 2, CONV_DIM])
    o_nps = dout("nps", [DEPTH, 2, 15, POOL_DIM])
    o_ncs = dout("ncs", [DEPTH, 2, 2, CONV_DIM])
    o_nvs = dout("nvs", [DEPTH, 64, SGU_DIM])

    P = Prog()
    es = ExitStack()

    def sb(name, shape, dt):
        t = es.enter_context(nc.sbuf_tensor(name, list(shape), dt))
        n = 1
        for s in shape[1:]:
            n *= s
        P.reg(name, n * _esize(dt))
        return t

    ident32 = sb("ident32", [128, 128], F32)
    identb = sb("identb", [128, 128], BF16)
    ones2048 = sb("ones2048", [128, 128], BF16)
    ones1024 = sb("ones1024", [128, 128], BF16)
    onesrow = sb("onesrow", [1, 128], F32)
    maskT = sb("maskT", [128, 128], F32)
    sguWT = sb("sguWT", [128, 16, 128], BF16)
    biasbc = sb("biasbc", [128, 16, 128], F32)
    poolwb = sb("poolwb", [128, 8, 128], BF16)
    params = sb("params", [128, DEPTH, 96], F32)
    invc = sb("invc", [128, 4, 16], F32)
    halo_pool = sb("halo_pool", [128, DEPTH, 4, 16], F32)
    halo_conv = sb("halo_conv", [128, DEPTH, 4, 2], F32)
    x32 = sb("x32", [128, 16, TP], F32)
    xb = sb("xb", [128, 16, TP], BF16)
    xin = sb("xin", [128, 2048], F32)
    xin2 = sb("xin2", [128, 2048], F32)
    pin = sb("pin", [128, 4, PLE], F32)
    pT = sb("pT", [128, 2, TP], BF16)
    pinf = pin[:, :, :].rearrange("p a b -> p (a b)")
    ring = sb("ring", [128, NB * SLOT_E], BF16)
    U = sb("U", [128, 44 * 512], BF16)
    MS = sb("MS", [128, 16, TP], BF16)
    lnt = sb("lnt", [128, 3, TP], F32)
    T4 = sb("T4", [128, 4, TP], F32)

    ps = []
    for i in range(8):
        t = es.enter_context(nc.psum_tensor(f"ps{i}", [128, 512], F32))
        P.reg(f"ps{i}", 2048, psum=True)
        ps.append(t)
    bank_ctr = [0]

    held = []

    def bank():
        while True:
            b = ps[bank_ctr[0] % 8]
            bank_ctr[0] += 1
            if not any(b is h for h in held):
                return b

    dma_sems = {}
    for i in range(NB):
        dma_sems[("ring", i)] = es.enter_context(nc.semaphore(f"sring{i}"))
    NSP = 16
    for i in range(NSP):
        dma_sems[("sp", i)] = es.enter_context(nc.semaphore(f"ssp{i}"))
    for i in range(4):
        dma_sems[("pl", i)] = es.enter_context(nc.semaphore(f"spl{i}"))
    sp_ctr = [0]
    pl_ctr = [0]
    out_dmas = []

    def dma_sp(out, in_, reads=(), writes=(), is_out=False):
        o = P.op("sp", lambda e: e.dma_start(out=out, in_=in_), reads=reads, writes=writes, dma=True)
        o.dma_sem = ("sp", sp_ctr[0] % NSP)
        sp_ctr[0] += 1
        if is_out:
            out_dmas.append(o)
        return o

    def dma_pool_small(out, in_, writes=()):
        o = P.op("pool", lambda e: e.dma_start(out=out, in_=in_), writes=writes, dma=True)
        o.dma_sem = ("pl", pl_ctr[0] % 4)
        pl_ctr[0] += 1
        return o

    def act(out, in_, func, bias=None, scale=None, extra_reads=()):
        kw = {}
        if bias is not None:
            kw["bias"] = bias
        if scale is not None:
            kw["scale"] = scale
        rd = [in_] + list(extra_reads)
        for v in (bias, scale):
            if v is not None and not isinstance(v, (int, float)):
                rd.append(v)
        return P.op("act", lambda e: e.activation(out=out, in_=in_, func=func, **kw), reads=rd, writes=[out])

    def tt(out, in0, in1, op):
        return P.op("dve", lambda e: e.tensor_tensor(out=out, in0=in0, in1=in1, op=op), reads=[in0, in1], writes=[out])

    def stt(out, in0, scalar, in1, op0, op1):
        rd = [in0, in1]
        if not isinstance(scalar, (int, float)):
            rd.append(scalar)
        return P.op("dve", lambda e: e.scalar_tensor_tensor(out=out, in0=in0, scalar=scalar, in1=in1, op0=op0, op1=op1),
                    reads=rd, writes=[out])

    def ts(out, in0, s1, op0):
        rd = [in0]
        if not isinstance(s1, (int, float)):
            rd.append(s1)
        return P.op("dve", lambda e: e.tensor_scalar(out=out, in0=in0, scalar1=s1, scalar2=None, op0=op0), reads=rd, writes=[out])

    def vcopy(out, in_):
        return P.op("dve", lambda e: e.tensor_copy(out=out, in_=in_), reads=[in_], writes=[out])

    def mm_group(out, pairs, reads, start_first=True, stop_last=True):
        def fn(e):
            ins = None
            n = len(pairs)
            for i, (l, r) in enumerate(pairs):
                ins = e.matmul(out, l, r, start=(start_first and i == 0), stop=(stop_last and i == n - 1))
            return ins
        return P.op("pe", fn, reads=reads, writes=[out])

    def transposes(items, reads, writes):
        def fn(e):
            ins = None
            for (o_, i_, id_) in items:
                ins = e.transpose(o_, i_, id_)
            return ins
        return P.op("pe", fn, reads=reads, writes=writes)

    blk_ctr = [0]
    NBLK_L = 156
    wscr_l = [nc.dram_tensor(f"wscr{l_}", [NBLK_L, 128, SLOT_E], BF16, kind="Internal").ap() for l_ in range(depth)]
    tile_blk = [0]
    first_tile = [True]
    tile_no_cur = [0]
    n_tiles_total = n_ptiles + (1 if with_sample else 0)
    wb_ops = {}

    def load_block(pieces):
        n = blk_ctr[0]
        blk_ctr[0] += 1
        bid = tile_blk[0]
        tile_blk[0] += 1
        s = n % NB
        base = s * SLOT_E
        views = []
        nel = 0
        for (eo, kc, ncols, src) in pieces:
            views.append(ring[:, base + eo: base + eo + kc * ncols].rearrange("p (k c) -> p k c", k=kc))
            nel = max(nel, eo + kc * ncols)
        whole = ring[:, base: base + nel]
        conv_tile = min((0, 1, 2, 1, 2, 0, 1, 2)[bid % 8], n_tiles_total - 1)
        if tile_no_cur[0] <= conv_tile:
            for (eo, kc, ncols, src), dst in zip(pieces, views):
                o = P.op("pool", (lambda d_, s_: (lambda e: e.dma_start(out=d_, in_=s_)))(dst, src), writes=[dst], dma=True)
                o.dma_sem = ("ring", s)
            if tile_no_cur[0] == conv_tile and n_tiles_total > 1:
                wb_ops[bid] = dma_sp(wscr_l[bid // NBLK_L][bid % NBLK_L, :, 0:nel], whole, reads=[whole])
        else:
            srcv = wscr_l[bid // NBLK_L][bid % NBLK_L, :, 0:nel]
            o = P.op("pool", (lambda d_, s_: (lambda e: e.dma_start(out=d_, in_=s_)))(whole, srcv), writes=[whole],
                     deps=[wb_ops[bid]], dma=True)
            o.dma_sem = ("ring", s)
        return views

    def wsrc(w_l, r0, nrows, c0, ncols):
        return w_l[r0:r0 + nrows, c0:c0 + ncols].rearrange("(k p) c -> p k c", p=128)

    o1 = dma_sp(ident32[:], c_ident[:, :], writes=[ident32[:]])
    dma_sp(maskT[:], c_mask[:, :], writes=[maskT[:]])
    dma_sp(invc[:], c_invc.rearrange("p (g t) -> p g t", g=4), writes=[invc[:]])
    vcopy(identb[:], ident32[:])
    P.op("dve", lambda e: e.memset(ones2048[:], 1.0 / 2048), writes=[ones2048[:]])
    P.op("dve", lambda e: e.memset(ones1024[:], 1.0 / 1024), writes=[ones1024[:]])
    P.op("dve", lambda e: e.memset(onesrow[:], 1.0), writes=[onesrow[:]])
    P.op("dve", lambda e: e.memset(halo_pool[:], 0.0), writes=[halo_pool[:]])
    P.op("dve", lambda e: e.memset(halo_conv[:], 0.0), writes=[halo_conv[:]])
    dma_pool_small(poolwb[:], pool_w.rearrange("l g c d -> c (l g) d"), writes=[poolwb[:]])
    stg = xin[:, :].rearrange("p (a b) -> p a b", a=16)
    dma_sp(stg, sgu_w.rearrange("l g t s -> t (l g) s"), writes=[xin[:, :]])
    for q in range(4):
        b = bank()
        transposes([(b[:, i * 128:(i + 1) * 128], stg[:, q * 4 + i, :], ident32[:]) for i in range(4)],
                   reads=[xin[:, :], ident32[:]], writes=[b[:, :]])
        tt(sguWT[:, q * 4:(q + 1) * 4, :], b[:, :].rearrange("p (a b) -> p a b", a=4),
           maskT[:].unsqueeze(1).broadcast_to([128, 4, 128]), ALU.mult)
    brow = xin[0:1, 0:2048]
    dma_sp(brow, sgu_b.rearrange("l g t -> (l g t)").unsqueeze(0), writes=[xin[:, :]])
    for q in range(4):
        b = bank()
        mm_group(b[:, :], [(onesrow[0:1, :], xin[0:1, q * 512:(q + 1) * 512])], reads=[onesrow[:], xin[:, :]])
        act(biasbc[:, q * 4:(q + 1) * 4, :], b[:, :].rearrange("p (a b) -> p a b", a=4), AF.Copy)
    PCOL = {"pool_scale": 0, "sgu_g": 4, "sgu_b": 12, "conv_w": 20, "ln1_g": 32, "ln1_b": 48, "ln2_g": 64, "ln2_b": 80}
    for l in range(depth):
        srcs = [(pool_scale, 0, 4), (sgu_ln_g, 4, 8), (sgu_ln_b, 12, 8), (ln1_g, 32, 16), (ln1_b, 48, 16),
                (ln2_g, 64, 16), (ln2_b, 80, 16)]
        for (t_, r0, nr) in srcs:
            dma_sp(xin[r0:r0 + nr, 0:128], t_[l].rearrange("(r c) -> r c", c=128), writes=[xin[:, 0:128]])
        dma_sp(xin[20:32, 0:128], conv_w[l].rearrange("k (r c) -> (k r) c", c=128), writes=[xin[:, 0:128]])
        b = bank()
        transposes([(b[:, 0:96], xin[0:96, 0:128], ident32[0:96, 0:96])], reads=[xin[:, 0:128], ident32[:]], writes=[b[:, :]])
        act(params[:, l, :], b[:, 0:96], AF.Copy)

    def pcol(l, name, k=0):
        c = PCOL[name] + k
        return params[:, l, c:c + 1]

    Uf = U[:, :]

    def uslot_bf(s0, nslots):
        return Uf[:, s0 * 512:(s0 + nslots) * 512]

    def uslot_f32(s0, nslots):
        return Uf[:, s0 * 512:(s0 + nslots) * 512].bitcast(F32)

    def dbg(name, ap):
        if not debug:
            return
        t = nc.dram_tensor("dbg_" + name, list(ap.shape), ap.dtype, kind="ExternalOutput").ap()
        dma_sp(t, ap, reads=[ap], is_out=True)

    tiles = []
    for i in range(NPT):
        tiles.append(dict(kind="p", idx=i, TT=TP, segs=[(0, TP)], L=TP, nseg=1))
    if with_sample:
        tiles.append(dict(kind="s", idx=0, TT=64, segs=[(0, 32), (32, 32)], L=32, nseg=2))

    def ln_finalize_a(A, B, TT, c0=0, c1=None):
        if c1 is None:
            c1 = TT
        ta = lnt[:, 1, c0:c1]
        tb = lnt[:, 2, c0:c1]
        act(ta, A[:, c0:c1], AF.Square)
        tt(ta, B[:, c0:c1], ta, ALU.subtract)
        act(tb, ta, AF.Sqrt, bias=epsc[:, 0:1])
        return tb

    def ln_finalize_b(tb, A, c0, c1):
        rstd = lnt[:, 0, c0:c1]
        P.op("dve", lambda e: e.reciprocal(out=rstd, in_=tb), reads=[tb], writes=[rstd])
        return rstd, A[:, c0:c1]

    def ln_finalize(A, B, TT, c0=0, c1=None):
        if c1 is None:
            c1 = TT
        tb = ln_finalize_a(A, B, TT, c0, c1)
        return ln_finalize_b(tb, A, c0, c1)

    epsc = sb("epsc", [128, 1], F32)
    P.op("dve", lambda e: e.memset(epsc[:], LN_EPS), writes=[epsc[:]])

    def ln_apply(src, gcol, bcol, rstd, Av, TT, outs, k):
        n = src.shape[-1]
        t1 = T4[:, (2 * k) % 4, :n]
        t2 = T4[:, (2 * k + 1) % 4, :n]
        tt(t1, src, Av, ALU.subtract)
        stt(t2, t1, gcol, rstd, ALU.mult, ALU.mult)
        for o_ in outs:
            act(o_, t2, AF.Identity, bias=bcol)

    def ln_full(tile_of, gname, bname, l, TT, halves, nout, ones_m, xbt, sqt, extra_out=None):
        banks = []
        for (h0, h1) in halves:
            A = bank()
            B = bank()
            banks.append((A, B))
            mm_group(A[:, h0:h1], [(ones_m[:], xbt[:, j, h0:h1]) for j in range(nout)],
                     reads=[ones_m[:]] + [xbt[:, j, h0:h1] for j in range(nout)])
            mm_group(B[:, h0:h1], [(ones_m[:], sqt[:, j, h0:h1]) for j in range(nout)],
                     reads=[ones_m[:]] + [sqt[:, j, h0:h1] for j in range(nout)])
        tbs = [ln_finalize_a(A, B, TT, h0, h1) for (h0, h1), (A, B) in zip(halves, banks)]
        for (h0, h1), (A, B), tb in zip(halves, banks, tbs):
            rstd, Av = ln_finalize_b(tb, A, h0, h1)
            for j in range(nout):
                outs = []
                if extra_out is not None:
                    outs.append(extra_out(j, h0, h1))
                outs.append(tile_of(j, h0, h1))
                ln_apply(tile_of(j, h0, h1), pcol(l, gname, j), pcol(l, bname, j), rstd, Av, TT, outs, j)

    prefetched = set()
    for tile_no, tile in enumerate(tiles):
        first_tile[0] = (tile_no == 0)
        tile_no_cur[0] = tile_no
        tile_blk[0] = 0
        TT = tile["TT"]
        L = tile["L"]
        nseg = tile["nseg"]
        is_p = tile["kind"] == "p"
        ti = tile["idx"]
        tok0 = ti * TP
        x_dram = xp if is_p else xsm
        y_dram = yp if is_p else ysm
        nsub = (TT + 127) // 128
        subs = [(s * 128, min(128, TT - s * 128)) for s in range(nsub)]

        for si_, (c0, nt) in enumerate(subs):
            xst = xin if si_ % 2 == 0 else xin2
            if (tile_no, si_) not in prefetched:
                dma_sp(xst[0:nt, :], x_dram[tok0 + c0: tok0 + c0 + nt, :], writes=[xst[:, :]])
            for dg in range(4):
                b = bank()
                transposes([(b[:, i * 128: i * 128 + nt], xst[0:nt, (dg * 4 + i) * 128:(dg * 4 + i + 1) * 128], ident32[0:nt, 0:nt])
                            for i in range(4)], reads=[xst[:, :], ident32[:]], writes=[b[:, :]])
                src = b[:, :].rearrange("p (a b) -> p a b", a=4)[:, :, 0:nt]
                act(x32[:, dg * 4:(dg + 1) * 4, c0:c0 + nt], src, AF.Copy)
                vcopy(xb[:, dg * 4:(dg + 1) * 4, c0:c0 + nt], x32[:, dg * 4:(dg + 1) * 4, c0:c0 + nt])

        for l in range(depth):
            assert tile_blk[0] == l * NBLK_L, (tile_blk[0], l)
            last_layer = (l == depth - 1)
            if last_layer and tile_no + 1 < len(tiles):
                nxt = tiles[tile_no + 1]
                n_is_p = nxt["kind"] == "p"
                n_dram = xp if n_is_p else xsm
                n_tok0 = nxt["idx"] * TP
                n_subs = [(s_ * 128, min(128, nxt["TT"] - s_ * 128)) for s_ in range((nxt["TT"] + 127) // 128)]
                for si_, (c0n, ntn) in enumerate(n_subs[:2]):
                    xst = xin if si_ % 2 == 0 else xin2
                    dma_sp(xst[0:ntn, :], n_dram[n_tok0 + c0n: n_tok0 + c0n + ntn, :], writes=[xst[:, :]])
                    prefetched.add((tile_no + 1, si_))
            wl_in = w_in[l]
            p_dram = pp[l] if is_p else psm[l]
            for si, (c0, nt) in enumerate(subs):
                dma_sp(pin[0:nt, si, :], p_dram[tok0 + c0: tok0 + c0 + nt, :], writes=[pin[:, si, :]])
                b = bank()
                transposes([(b[:, k * 128:k * 128 + nt], pin[0:nt, si, k * 128:(k + 1) * 128], ident32[0:nt, 0:nt]) for k in range(2)],
                           reads=[pin[:, si, :], ident32[:]], writes=[b[:, :]])
                vcopy(pT[:, :, c0:c0 + nt], b[:, 0:256].rearrange("p (a b) -> p a b", a=2)[:, :, 0:nt])

            xb_all = xb[:, :, :TT]

            def win_block(ct0):
                return load_block([(0, 16, 256, wsrc(wl_in, 0, D, ct0 * 128, 256))])[0]

            def mm16(bk, wv, jj, rhs_t, nk=16, c0=0, c1=None):
                if c1 is None:
                    c1 = TT
                return mm_group(bk[:, c0:c1], [(wv[:, k, jj * 128:(jj + 1) * 128], rhs_t[:, k, c0:c1]) for k in range(nk)],
                                reads=[wv] + [rhs_t[:, k, c0:c1] for k in range(nk)])

            halves = [(0, TT // 2), (TT // 2, TT)] if is_p else [(0, TT)]

            gv32 = uslot_f32(0, 16).rearrange("p (a b) -> p a b", a=8)
            vnb = uslot_bf(16, 8).rearrange("p (a b) -> p a b", a=8)
            vT = uslot_bf(24, 8).rearrange("p (c g d) -> p c g d", c=4, g=8)
            y_b = uslot_bf(32, 8).rearrange("p (a b) -> p a b", a=8)
            y_a = uslot_bf(40, 4).rearrange("p (a b) -> p a b", a=4)
            sqv = MS[:, 0:8, :]
            vblocks = [win_block(12 + 2 * bi) for bi in range(4)]
            for (h0, h1) in halves:
                for bi in range(4):
                    wv = vblocks[bi]
                    for jj in range(2):
                        i = 2 * bi + jj
                        bk = bank()
                        mm16(bk, wv, jj, xb, c0=h0, c1=h1)
                        act(gv32[:, i, h0:h1], bk[:, h0:h1], AF.Gelu)
                        vcopy(vnb[:, i, h0:h1], gv32[:, i, h0:h1])
                        act(sqv[:, i, h0:h1], gv32[:, i, h0:h1], AF.Square)
            A = bank()
            mm_group(A[:, :TT], [(ones1024[:], vnb[:, i, :TT]) for i in range(8)], reads=[ones1024[:], vnb[:, :, :TT]])
            B = bank()
            mm_group(B[:, :TT], [(ones1024[:], sqv[:, i, :TT]) for i in range(8)], reads=[ones1024[:], sqv[:, :, :TT]])
            rstd, nmr = ln_finalize(A, B, TT)
            held.append(A)
            ub = MS[:, 8:16, :]
            for bi in range(4):
                wv = win_block(4 + 2 * bi)
                for jj in range(2):
                    g = 2 * bi + jj
                    bk = bank()
                    mm16(bk, wv, jj, xb)
                    act(ub[:, g, :TT], bk[:, :TT], AF.Gelu)
                    for (h0, h1) in halves:
                        outs = [vnb[:, g, h0:h1]]
                        if not is_p:
                            outs.append(gv32[:, g, h0:h1])
                        ln_apply(gv32[:, g, h0:h1], pcol(l, "sgu_g", g), pcol(l, "sgu_b", g), rstd[:, h0:h1], nmr[:, h0:h1],
                                 TT, outs, g)
            held.clear()
            if not is_p:
                for half in range(2):
                    b = bank()
                    transposes([(b[0:TT, i * 128:(i + 1) * 128], gv32[:, half * 4 + i, :TT], ident32[:]) for i in range(4)],
                               reads=[gv32[:, half * 4:(half + 1) * 4, :TT], ident32[:]], writes=[b[:, :]])
                    act(pinf[0:TT, half * 512:(half + 1) * 512], b[0:TT, :], AF.Copy)
                dma_sp(o_nvs[l], pinf[0:TT, 0:1024], reads=[pinf[:, 0:1024]], is_out=True)
            if is_p:
                chunks = [(c * 128, 128) for c in range(TT // 128)]
            else:
                chunks = [(0, 32), (32, 32)]
            for ci, (c0, nt) in enumerate(chunks):
                b = bank()
                bb = b[:, :].bitcast(BF16)
                transposes([(bb[0:nt, g * 128:(g + 1) * 128], vnb[:, g, c0:c0 + nt], identb[:]) for g in range(8)],
                           reads=[vnb[:, g, c0:c0 + nt] for g in range(8)] + [identb[:]], writes=[b[:, :]])
                vcopy(vT[0:nt, ci, :, :], bb[0:nt, :].rearrange("p (g d) -> p g d", g=8))
            if (not is_p) and l == 0:
                dbg("sguWT", sguWT[:, :, :]); dbg("biasbc", biasbc[:, :, :]); dbg("vnb", vnb[:, :, :TT]); dbg("vT", vT[0:32, 0:2, :, :])
            for g in range(8):
                bs = bank()
                def fn(e, bs=bs, g=g, chunks=chunks, vT=vT, l=l):
                    ins = None
                    for ci, (c0, nt) in enumerate(chunks):
                        ins = e.matmul(bs[:, c0:c0 + nt], vT[0:nt, ci, g, :], sguWT[0:nt, l * 8 + g, 0:nt], start=True, stop=True)
                    return ins
                P.op("pe", fn, reads=[vT[:, :, g, :], sguWT[:, l * 8 + g, :]], writes=[bs[:, :]])
                tmp = T4[:, g % 4, :TT]
                for ci, (c0, nt) in enumerate(chunks):
                    tt(tmp[:, c0:c0 + nt], bs[:, c0:c0 + nt], biasbc[:, l * 8 + g, 0:nt], ALU.add)
                tt(y_b[:, g, :TT], tmp, ub[:, g, :TT], ALU.mult)

            SW = 16 + L
            W = nseg * SW
            a32 = uslot_f32(0, 9)[:, 0:4 * W].rearrange("p (g w) -> p g w", g=4)
            a32s = uslot_f32(0, 9)[:, 0:4 * W].rearrange("p (g s w) -> p g s w", g=4, s=nseg)
            t1 = uslot_f32(9, 3)[:, 0:W]
            t2 = uslot_f32(12, 3)[:, 0:W]
            pooled_bufs = [uslot_bf(sl_, 1)[:, 0:TT] for sl_ in (15, 29, 30, 31)]
            if is_p:
                vcopy(a32s[:, :, 0, 0:16], halo_pool[:, l, :, :])
            else:
                for s in range(2):
                    dma_sp(pinf[0:15, 0:512], st_pool[l, s], writes=[pinf[:, 0:512]])
                    b = bank()
                    transposes([(b[:, g * 16:g * 16 + 15], pinf[0:15, g * 128:(g + 1) * 128], ident32[0:15, 0:15]) for g in range(4)],
                               reads=[pinf[:, 0:512], ident32[:]], writes=[b[:, :]])
                    vcopy(a32s[:, :, s, 1:16], b[:, 0:64].rearrange("p (g t) -> p g t", g=4)[:, :, 0:15])
            for bi in range(2):
                wv = win_block(2 * bi)
                for jj in range(2):
                    g = 2 * bi + jj
                    bk = bank()
                    mm16(bk, wv, jj, xb)
                    act(a32s[:, g, :, 16:16 + L], bk[:, :TT].rearrange("p (s t) -> p s t", s=nseg), AF.Copy)
            cb32 = MS[:, 0:8, :].rearrange("p a b -> p (a b)").bitcast(F32).rearrange("p (a b) -> p a b", a=4)
            for bi in range(2):
                wv = win_block(20 + 2 * bi)
                for jj in range(2):
                    c = 2 * bi + jj
                    bk = bank()
                    mm16(bk, wv, jj, xb)
                    act(cb32[:, c, :TT], bk[:, :TT], AF.Copy)
            for g in range(4):
                win = WINS[g]
                src = a32[:, g, :]
                cur = src
                sh = 1
                k = 0
                while sh < win:
                    dst = t1 if (k % 2 == 0) else t2
                    lo = 2 * sh - 1
                    tt(dst[:, lo:W], cur[:, lo:W], cur[:, lo - sh:W - sh], ALU.add)
                    cur = dst
                    sh *= 2
                    k += 1
                curs = cur.rearrange("p (s w) -> p s w", s=nseg)[:, :, 16:16 + L]
                pooled = pooled_bufs[g]
                stt(pooled.rearrange("p (s t) -> p s t", s=nseg), curs, 1.0 / win, a32s[:, g, :, 16:16 + L], ALU.mult, ALU.subtract)
                if is_p and ti == 0:
                    tmp16 = T4[:, 0, 0:16]
                    tt(tmp16, cur[:, 16:32], invc[:, g, :], ALU.mult)
                    tt(pooled[:, 0:16], tmp16, a32[:, g, 16:32], ALU.subtract)
                bk = bank()
                mm_group(bk[:, :TT], [(poolwb[:, l * 4 + g, :], pooled)], reads=[poolwb[:, l * 4 + g, :], pooled])
                act(y_a[:, g, :TT], bk[:, :TT], AF.Copy, scale=pcol(l, "pool_scale", g))
            if is_p:
                if ti < NPT - 1:
                    vcopy(halo_pool[:, l, :, 1:16], a32s[:, :, 0, L + 1:L + 16])
                else:
                    b = bank()
                    transposes([(b[0:15, g * 128:(g + 1) * 128], a32s[:, g, 0, L + 1:L + 16], ident32[:]) for g in range(4)],
                               reads=[a32[:, :, :], ident32[:]], writes=[b[:, :]])
                    act(pinf[0:15, 0:512], b[0:15, :], AF.Copy)
                    dma_sp(o_npp[l], pinf[0:15, 0:512], reads=[pinf[:, 0:512]], is_out=True)
            else:
                for s in range(2):
                    b = bank()
                    transposes([(b[0:15, g * 128:(g + 1) * 128], a32s[:, g, s, L + 1:L + 16], ident32[:]) for g in range(4)],
                               reads=[a32[:, :, :], ident32[:]], writes=[b[:, :]])
                    act(pinf[0:15, 0:512], b[0:15, :], AF.Copy)
                    dma_sp(o_nps[l, s], pinf[0:15, 0:512], reads=[pinf[:, 0:512]], is_out=True)

            CW = 2 + L
            WC = nseg * CW
            cc32 = uslot_f32(8, 8).rearrange("p (a b) -> p a b", a=4)
            ci32 = uslot_f32(16, 9)[:, 0:4 * WC].rearrange("p (c s w) -> p c s w", c=4, s=nseg)
            accs = uslot_f32(25, 4).rearrange("p (a b) -> p a b", a=2)
            y_c = uslot_bf(8, 8).rearrange("p (a two b) -> p a two b", a=4, two=2)[:, :, 0, :]
            if is_p:
                vcopy(ci32[:, :, 0, 0:2], halo_conv[:, l, :, :])
            else:
                for s in range(2):
                    dma_sp(pinf[0:2, 0:512], st_conv[l, s], writes=[pinf[:, 0:512]])
                    b = bank()
                    transposes([(b[:, c * 2:c * 2 + 2], pinf[0:2, c * 128:(c + 1) * 128], ident32[0:2, 0:2]) for c in range(4)],
                               reads=[pinf[:, 0:512], ident32[:]], writes=[b[:, :]])
                    vcopy(ci32[:, :, s, 0:2], b[:, 0:8].rearrange("p (c t) -> p c t", c=4))
            for bi in range(2):
                wv = win_block(24 + 2 * bi)
                for jj in range(2):
                    c = 2 * bi + jj
                    bk = bank()
                    mm16(bk, wv, jj, xb)
                    act(cc32[:, c, :TT], bk[:, :TT], AF.Copy)
            for bi in range(2):
                wv = win_block(28 + 2 * bi)
                for jj in range(2):
                    c = 2 * bi + jj
                    bk = bank()
                    mm16(bk, wv, jj, xb)
                    tt(ci32[:, c, :, 2:2 + L], bk[:, :TT].rearrange("p (s t) -> p s t", s=nseg),
                       cc32[:, c, :TT].rearrange("p (s t) -> p s t", s=nseg), ALU.mult)
                    acc = accs[:, c % 2, :TT].rearrange("p (s t) -> p s t", s=nseg)
                    ts(acc, ci32[:, c, :, 0:L], pcol(l, "conv_w", 0 * 4 + c), ALU.mult)
                    stt(acc, ci32[:, c, :, 1:1 + L], pcol(l, "conv_w", 1 * 4 + c), acc, ALU.mult, ALU.add)
                    stt(acc, ci32[:, c, :, 2:2 + L], pcol(l, "conv_w", 2 * 4 + c), acc, ALU.mult, ALU.add)
                    tt(y_c[:, c, :TT], cb32[:, c, :TT], accs[:, c % 2, :TT], ALU.mult)
            if is_p:
                if ti < NPT - 1:
                    vcopy(halo_conv[:, l, :, :], ci32[:, :, 0, L:L + 2])
                else:
                    b = bank()
                    transposes([(b[0:2, c * 128:(c + 1) * 128], ci32[:, c, 0, L:L + 2], ident32[:]) for c in range(4)],
                               reads=[uslot_f32(16, 9), ident32[:]], writes=[b[:, :]])
                    act(pinf[0:2, 0:512], b[0:2, :], AF.Copy)
                    dma_sp(o_ncp[l], pinf[0:2, 0:512], reads=[pinf[:, 0:512]], is_out=True)
            else:
                for s in range(2):
                    b = bank()
                    transposes([(b[0:2, c * 128:(c + 1) * 128], ci32[:, c, s, L:L + 2], ident32[:]) for c in range(4)],
                               reads=[uslot_f32(16, 9), ident32[:]], writes=[b[:, :]])
                    act(pinf[0:2, 0:512], b[0:2, :], AF.Copy)
                    dma_sp(o_ncs[l, s], pinf[0:2, 0:512], reads=[pinf[:, 0:512]], is_out=True)

            if (not is_p) and l == 0:
                dbg("ya", y_a[:, :, :TT]); dbg("yb", y_b[:, :, :TT]); dbg("yc", y_c[:, :, :TT])
            merged = MS
            branches = [(w_br_a[l], 4, y_a, 0), (w_br_b[l], 8, y_b, 1), (w_br_c[l], 4, y_c, 2)]
            for jp in range(8):
                for bi_, (wbr, nk, ysrc, gi) in enumerate(branches):
                    gv = load_block([(0, 16, 256, wsrc(wl_in, 0, D, OFF_G + gi * D + jp * 256, 256))])[0]
                    bv = load_block([(0, nk, 256, wsrc(wbr, 0, nk * 128, jp * 256, 256))])[0]
                    for jj in range(2):
                        j = jp * 2 + jj
                        m32 = T4[:, 2 + jj, :TT]
                        bg = bank()
                        mm16(bg, gv, jj, xb)
                        gsb = T4[:, jj, :TT]
                        act(gsb, bg[:, :TT], AF.Sigmoid)
                        bb_ = bank()
                        mm16(bb_, bv, jj, ysrc, nk=nk)
                        if bi_ == 0:
                            tt(m32, bb_[:, :TT], gsb, ALU.mult)
                        elif bi_ == 1:
                            tt(gsb, bb_[:, :TT], gsb, ALU.mult)
                            tt(m32, m32, gsb, ALU.add)
                        else:
                            tt(gsb, bb_[:, :TT], gsb, ALU.mult)
                            tt(merged[:, j, :TT], m32, gsb, ALU.add)

            if (not is_p) and l == 0:
                dbg("merged", merged[:, :, :TT])
            sq1 = uslot_bf(0, 16).rearrange("p (a b) -> p a b", a=16)
            for jp in range(8):
                wv = load_block([(0, 16, 256, wsrc(w_o[l], 0, D, jp * 256, 256))])[0]
                for jj in range(2):
                    j = jp * 2 + jj
                    bk = bank()
                    mm16(bk, wv, jj, merged)
                    stt(x32[:, j, :TT], x32[:, j, :TT], ALPHA, bk[:, :TT], ALU.mult, ALU.add)
                    vcopy(xb[:, j, :TT], x32[:, j, :TT])
                    act(sq1[:, j, :TT], x32[:, j, :TT], AF.Square)
            ln_full(lambda j, a, b: x32[:, j, a:b], "ln1_g", "ln1_b", l, TT, halves, 16, ones2048, xb, sq1,
                    extra_out=lambda j, a, b: xb[:, j, a:b])
            if (not is_p) and l == 0:
                dbg("x1", x32[:, :, :TT])
            actb = U[:, :].rearrange("p (a b) -> p a b", a=44)
            def ffn_pair(p_, wg, wu, h0, h1):
                for jj in range(2):
                    f = 2 * p_ + jj
                    bg = bank()
                    mm16(bg, wg, jj, xb, c0=h0, c1=h1)
                    sg = T4[:, f % 4, h0:h1]
                    act(sg, bg[:, h0:h1], AF.Silu)
                    bu = bank()
                    mm16(bu, wu, jj, xb, c0=h0, c1=h1)
                    tt(actb[:, f, h0:h1], bu[:, h0:h1], sg, ALU.mult)

            first_pairs = []
            for p_ in range(2):
                wg = load_block([(0, 16, 256, wsrc(w_gu[l], 0, D, p_ * 256, 256))])[0]
                wu = load_block([(0, 16, 256, wsrc(w_gu[l], 0, D, D_FF + p_ * 256, 256))])[0]
                first_pairs.append((p_, wg, wu))
            for (h0, h1) in halves:
                for (p_, wg, wu) in first_pairs:
                    ffn_pair(p_, wg, wu, h0, h1)
            for p_ in range(2, 22):
                wg = load_block([(0, 16, 256, wsrc(w_gu[l], 0, D, p_ * 256, 256))])[0]
                wu = load_block([(0, 16, 256, wsrc(w_gu[l], 0, D, D_FF + p_ * 256, 256))])[0]
                ffn_pair(p_, wg, wu, 0, TT)
            if (not is_p) and l == 0:
                dbg("act", actb[:, :, :TT])
            for jp in range(8):
                vs = load_block([(0, 16, 256, wsrc(w_pe_gate[l], 0, D, jp * 256, 256)),
                                 (16 * 256, 2, 256, wsrc(w_pe[l], 0, PLE, jp * 256, 256))])
                wpg, wpe = vs
                for jj in range(2):
                    j = jp * 2 + jj
                    bg = bank()
                    mm16(bg, wpg, jj, xb)
                    sgt = T4[:, j % 4, :TT]
                    act(sgt, bg[:, :TT], AF.Sigmoid)
                    be = bank()
                    mm16(be, wpe, jj, pT, nk=2)
                    tt(sgt, be[:, :TT], sgt, ALU.mult)
                    stt(x32[:, j, :TT], x32[:, j, :TT], ALPHA, sgt, ALU.mult, ALU.add)
            if (not is_p) and l == 0:
                dbg("xple", x32[:, :, :TT])
            sq2 = MS
            for jp in range(8):
                bks = [bank(), bank()]
                for q in range(4):
                    wv = load_block([(0, 11, 256, wsrc(w_down[l], q * 1408, 1408, jp * 256, 256))])[0]
                    def fn(e, wv=wv, q=q, bks=bks, TT=TT, actb=actb):
                        ins = None
                        for k in range(11):
                            for jj in range(2):
                                ins = e.matmul(bks[jj][:, :TT], wv[:, k, jj * 128:(jj + 1) * 128], actb[:, q * 11 + k, :TT],
                                               start=(q == 0 and k == 0), stop=(q == 3 and k == 10))
                        return ins
                    P.op("pe", fn, reads=[wv, actb[:, q * 11:(q + 1) * 11, :TT]], writes=[bks[0][:, :TT], bks[1][:, :TT]])
                for jj in range(2):
                    j = jp * 2 + jj
                    tt(x32[:, j, :TT], bks[jj][:, :TT], x32[:, j, :TT], ALU.add)
                    vcopy(xb[:, j, :TT], x32[:, j, :TT])
                    act(sq2[:, j, :TT], x32[:, j, :TT], AF.Square)
            if (not is_p) and l == 0:
                dbg("xpre2", x32[:, :, :TT])
            ln_full(lambda j, a, b: x32[:, j, a:b], "ln2_g", "ln2_b", l, TT, halves, 16, ones2048, xb, sq2,
                    extra_out=(None if last_layer else (lambda j, a, b: xb[:, j, a:b])))

        ost = [T4[:, :, :].rearrange("p a b -> p (a b)"), MS[:, :, :].rearrange("p a b -> p (a b)").bitcast(F32)[:, 0:2048]]
        for si_, (c0, nt) in enumerate(subs):
            xst = ost[si_ % 2]
            for dg in range(4):
                b = bank()
                transposes([(b[0:nt, i * 128:(i + 1) * 128], x32[:, dg * 4 + i, c0:c0 + nt], ident32[:]) for i in range(4)],
                           reads=[x32[:, dg * 4 + i, c0:c0 + nt] for i in range(4)] + [ident32[:]], writes=[b[:, :]])
                if dg % 2 == 0:
                    act(xst[0:nt, dg * 512:(dg + 1) * 512], b[0:nt, :], AF.Copy)
                else:
                    vcopy(xst[0:nt, dg * 512:(dg + 1) * 512], b[0:nt, :])
            dma_sp(y_dram[tok0 + c0: tok0 + c0 + nt, :], xst[0:nt, :], reads=[xst[:, :]], is_out=True)

    P.op("sp", lambda e: None, deps=out_dmas)
    P.emit(nc, es, None, dma_sems)
    es.close()
    return nc, P


_CACHE = {}


def _consts():
    ident = np.eye(128, dtype=np.float32)
    s = np.arange(128)[:, None]
    t = np.arange(128)[None, :]
    mask = (s <= t).astype(np.float32)
    invc = np.zeros((4, 16), np.float32)
    for g, w in enumerate(WINS):
        for tt_ in range(16):
            invc[g, tt_] = 1.0 / min(tt_ + 1, w)
    invc = np.broadcast_to(invc.reshape(1, 64), (128, 64)).copy()
    return ident, mask, invc


def kernel(**inputs):
    inp = {k: np.asarray(v) for k, v in inputs.items()}
    if "nc" not in _CACHE:
        _CACHE["nc"] = build_program()[0]
    nc = _CACHE["nc"]
    ident, mask, invc = _consts()
    wnames = ["w_in", "pool_w", "pool_scale", "sgu_ln_g", "sgu_ln_b", "sgu_w", "sgu_b", "conv_w", "w_br_a", "w_br_b",
              "w_br_c", "w_o", "ln1_g", "ln1_b", "w_gu", "w_down", "w_pe", "w_pe_gate", "ln2_g", "ln2_b"]
    shared = {k: np.ascontiguousarray(inp[k], dtype=np.float32) for k in wnames}
    in_maps = []
    for c in range(NCORES):
        m = dict(shared)
        m["xp"] = np.ascontiguousarray(inp["x_prompt"][c])
        m["xs"] = np.ascontiguousarray(inp["x_sample"][2 * c:2 * c + 2].reshape(64, D))
        m["st_pool"] = np.ascontiguousarray(inp["state_pool"][:, 2 * c:2 * c + 2])
        m["st_conv"] = np.ascontiguousarray(inp["state_conv"][:, 2 * c:2 * c + 2])
        m["pp"] = np.ascontiguousarray(inp["p_prompt"][:, c])
        m["ps"] = np.ascontiguousarray(inp["p_sample"][:, 2 * c:2 * c + 2].reshape(DEPTH, 64, PLE))
        m["c_ident"] = ident
        m["c_mask"] = mask
        m["c_invc"] = invc
        in_maps.append(m)
    res = run_bass_kernel_spmd(nc, in_maps, core_ids=list(range(NCORES)))
    rs = res.results
    y_prompt = np.stack([rs[c]["yp"] for c in range(NCORES)], 0)
    y_sample = np.concatenate([rs[c]["ys"].reshape(2, DEC_SEQ, D) for c in range(NCORES)], 0)
    npp = np.stack([rs[c]["npp"] for c in range(NCORES)], 1)
    ncp = np.stack([rs[c]["ncp"] for c in range(NCORES)], 1)
    nps = np.concatenate([rs[c]["nps"] for c in range(NCORES)], 1)
    ncs = np.concatenate([rs[c]["ncs"] for c in range(NCORES)], 1)
    nvs = np.concatenate([rs[c]["nvs"].reshape(DEPTH, 2, DEC_SEQ, SGU_DIM) for c in range(NCORES)], 1)
    return (y_prompt.astype(np.float32), y_sample.astype(np.float32), npp.astype(np.float32), ncp.astype(np.float32),
            nps.astype(np.float32), ncs.astype(np.float32), nvs.astype(np.float32))
```
